# Optimizing a Trainium2 kernel written in Bass

```python
import jax
import jax.numpy as jnp
from jax import lax
import numpy as np

D_MODEL = 2048
BATCH = 2
SEQ = 4096
DEPTH = 2

GRID_W = 64
CTX_LEN = 256
N_EVEN = (DEPTH + 1) // 2
N_ODD = DEPTH // 2
EPS = 1e-6
N_MOD = 6

CHUNK = 128
A_HEADS = 8
A_HEAD_DIM = 128
A_WIDTH = A_HEADS * A_HEAD_DIM
B_WIDTH = 1024
B_CONV = 31
MIX_IN = 2 * A_WIDTH + 2 * B_WIDTH
MIX_OUT = A_WIDTH + B_WIDTH

MLA_HEADS = 16
Q_LORA = 768
KV_LORA = 512
QK_NOPE = 128
QK_ROPE = 64
V_DIM = 128
ROPE_THETA = 10000.0
Q_BLOCK = 128
MLA_IN = Q_LORA + KV_LORA + QK_ROPE

D_FF = 5632
FFN_CONV = 3

kernel_name = 'hybrid_gmlp_conformer_mla_dit_block'


def rmsnorm(x, g):
    xf = x.astype(jnp.float32)
    y = xf * lax.rsqrt(jnp.mean(xf * xf, axis=-1, keepdims=True) + EPS)
    return (y * g.astype(jnp.float32)).astype(x.dtype)


def layernorm(x, g, b):
    xf = x.astype(jnp.float32)
    mu = jnp.mean(xf, axis=-1, keepdims=True)
    var = jnp.mean(jnp.square(xf - mu), axis=-1, keepdims=True)
    y = (xf - mu) * lax.rsqrt(var + EPS)
    return (y * g.astype(jnp.float32) + b.astype(jnp.float32)).astype(x.dtype)


def modulate(h, shift, scale):
    return h * (1 + scale) + shift


def adaln(cvec, w, b):
    m = jax.nn.silu(cvec) @ w + b
    return jnp.split(m, N_MOD, axis=-1)


def depthwise_conv(x, w, b):
    pad = (w.shape[0] - 1) // 2
    y = lax.conv_general_dilated(x, w[:, None, :].astype(x.dtype), window_strides=(1,), padding=[(pad, pad)], dimension_numbers=('NWC', 'WIO', 'NWC'), feature_group_count=x.shape[-1])
    return y + b.astype(x.dtype)


def chunk_gmlp(z, ln_g, ln_b, w_s, b_s):
    z = jax.nn.gelu(z)
    u, v = z[..., :A_WIDTH], z[..., A_WIDTH:]
    v = layernorm(v, ln_g, ln_b)
    bn, L, _ = v.shape
    v = v.reshape(bn, L // CHUNK, CHUNK, A_HEADS, A_HEAD_DIM)
    v = jnp.einsum('hij,bnjhd->bnihd', w_s.astype(v.dtype), v) + b_s.T.astype(v.dtype)[:, :, None]
    return u * v.reshape(bn, L, A_WIDTH)


def conformer_conv(z, conv_w, conv_b, ln_g, ln_b):
    a, g = z[..., :B_WIDTH], z[..., B_WIDTH:]
    h = a * jax.nn.sigmoid(g)
    h = depthwise_conv(h, conv_w, conv_b)
    h = layernorm(h, ln_g, ln_b)
    return jax.nn.silu(h)


def ab_mixer(h, w_in, b_in, a_ln_g, a_ln_b, a_w_s, a_b_s, b_conv_w, b_conv_b, b_ln_g, b_ln_b, w_out):
    z = h @ w_in + b_in
    ya = chunk_gmlp(z[..., :2 * A_WIDTH], a_ln_g, a_ln_b, a_w_s, a_b_s)
    yb = conformer_conv(z[..., 2 * A_WIDTH:], b_conv_w, b_conv_b, b_ln_g, b_ln_b)
    return jnp.concatenate([ya, yb], axis=-1) @ w_out


def conv_ffn(h, w_up, conv_w, conv_b, w_down):
    z = h @ w_up
    g, u = z[..., :D_FF], z[..., D_FF:]
    g = depthwise_conv(g, conv_w, conv_b)
    return (jax.nn.silu(g) * u) @ w_down


def axial_rope(L):
    rows = L // GRID_W
    row = jnp.repeat(jnp.arange(rows, dtype=jnp.float32), GRID_W)
    col = jnp.tile(jnp.arange(GRID_W, dtype=jnp.float32), rows)
    n_freq = QK_ROPE // 4
    inv = ROPE_THETA ** (-jnp.arange(n_freq, dtype=jnp.float32) / n_freq)
    ang = jnp.concatenate([row[:, None] * inv, col[:, None] * inv], axis=-1)
    return jnp.cos(ang), jnp.sin(ang)


def apply_rope(x, cos, sin):
    half = x.shape[-1] // 2
    x1, x2 = x[..., :half], x[..., half:]
    cos = cos.astype(x.dtype)
    sin = sin.astype(x.dtype)
    return jnp.concatenate([x1 * cos - x2 * sin, x2 * cos + x1 * sin], axis=-1)


def mla_queries(cq, q_norm_g, w_uq, cos, sin):
    bn, L, _ = cq.shape
    q = (rmsnorm(cq, q_norm_g) @ w_uq).reshape(bn, L, MLA_HEADS, QK_NOPE + QK_ROPE)
    q_nope, q_pe = q[..., :QK_NOPE], q[..., QK_NOPE:]
    if cos is not None:
        q_pe = apply_rope(q_pe, cos[None, :, None, :], sin[None, :, None, :])
    return q_nope, q_pe


def mla_keys_values(ckv, k_pe, kv_norm_g, w_ukv, cos, sin):
    bn, L, _ = ckv.shape
    kv = (rmsnorm(ckv, kv_norm_g) @ w_ukv).reshape(bn, L, MLA_HEADS, QK_NOPE + V_DIM)
    k_nope, v = kv[..., :QK_NOPE], kv[..., QK_NOPE:]
    if cos is not None:
        k_pe = apply_rope(k_pe, cos[None], sin[None])
    return k_nope, k_pe, v


def attend(q_nope, q_pe, k_nope, k_pe, v):
    bn, lq, _, _ = q_nope.shape
    nb = lq // Q_BLOCK
    qn = q_nope.reshape(bn, nb, Q_BLOCK, MLA_HEADS, QK_NOPE).transpose(1, 0, 2, 3, 4)
    qp = q_pe.reshape(bn, nb, Q_BLOCK, MLA_HEADS, QK_ROPE).transpose(1, 0, 2, 3, 4)
    scale = (QK_NOPE + QK_ROPE) ** -0.5

    def block(args):
        qn_b, qp_b = args
        s = jnp.einsum('bqhd,bkhd->bhqk', qn_b, k_nope) + jnp.einsum('bqhd,bkd->bhqk', qp_b, k_pe)
        p = jax.nn.softmax(s.astype(jnp.float32) * scale, axis=-1).astype(v.dtype)
        return jnp.einsum('bhqk,bkhd->bqhd', p, v)

    o = lax.map(block, (qn, qp))
    return o.transpose(1, 0, 2, 3, 4).reshape(bn, lq, MLA_HEADS * V_DIM)


def mla_mixer(h_lat, h_ctx, w_in, q_norm_g, w_uq, kv_norm_g, w_ukv, w_o, cos, sin, ctx_out):
    z = h_lat @ w_in
    cq, ckv, kpe = z[..., :Q_LORA], z[..., Q_LORA:Q_LORA + KV_LORA], z[..., Q_LORA + KV_LORA:]
    zc = h_ctx @ w_in[:, Q_LORA:]
    ckv_c, kpe_c = zc[..., :KV_LORA], zc[..., KV_LORA:]
    kn_l, kp_l, v_l = mla_keys_values(ckv, kpe, kv_norm_g, w_ukv, cos, sin)
    kn_c, kp_c, v_c = mla_keys_values(ckv_c, kpe_c, kv_norm_g, w_ukv, None, None)
    qn, qp = mla_queries(cq, q_norm_g, w_uq, cos, sin)
    o_lat = attend(qn, qp, jnp.concatenate([kn_l, kn_c], axis=1), jnp.concatenate([kp_l, kp_c], axis=1), jnp.concatenate([v_l, v_c], axis=1)) @ w_o
    o_ctx = None
    if ctx_out:
        qn_c, qp_c = mla_queries(h_ctx @ w_in[:, :Q_LORA], q_norm_g, w_uq, None, None)
        o_ctx = attend(qn_c, qp_c, kn_c, kp_c, v_c) @ w_o
    return o_lat, o_ctx


def setup_inputs(seed: int = 0) -> dict:
    key = jax.random.key(seed)
    ks = iter(jax.random.split(key, 40))
    f32 = jnp.float32

    def nrm(shape, scale):
        return jax.random.normal(next(ks), shape, f32) * scale

    def gain(shape):
        return 1.0 + nrm(shape, 0.02)

    return {
        'x': nrm((BATCH, SEQ, D_MODEL), 1.0),
        'c': nrm((BATCH, D_MODEL), 1.0),
        'ctx': nrm((BATCH, CTX_LEN, D_MODEL), 1.0),
        'c_ctx': nrm((D_MODEL,), 1.0),
        'norm1_g': gain((DEPTH, D_MODEL)),
        'norm2_g': gain((DEPTH, D_MODEL)),
        'w_ada': nrm((DEPTH, D_MODEL, N_MOD * D_MODEL), D_MODEL ** -0.5),
        'b_ada': nrm((DEPTH, N_MOD * D_MODEL), 0.02),
        'ab_w_in': nrm((N_EVEN, D_MODEL, MIX_IN), D_MODEL ** -0.5),
        'ab_b_in': nrm((N_EVEN, MIX_IN), 0.02),
        'a_ln_g': gain((N_EVEN, A_WIDTH)),
        'a_ln_b': nrm((N_EVEN, A_WIDTH), 0.02),
        'a_w_s': nrm((N_EVEN, A_HEADS, CHUNK, CHUNK), CHUNK ** -0.5),
        'a_b_s': gain((N_EVEN, A_HEADS, CHUNK)),
        'b_conv_w': nrm((N_EVEN, B_CONV, B_WIDTH), B_CONV ** -0.5),
        'b_conv_b': nrm((N_EVEN, B_WIDTH), 0.02),
        'b_ln_g': gain((N_EVEN, B_WIDTH)),
        'b_ln_b': nrm((N_EVEN, B_WIDTH), 0.02),
        'ab_w_out': nrm((N_EVEN, MIX_OUT, D_MODEL), MIX_OUT ** -0.5),
        'mla_w_in': nrm((N_ODD, D_MODEL, MLA_IN), D_MODEL ** -0.5),
        'mla_q_norm_g': gain((N_ODD, Q_LORA)),
        'mla_w_uq': nrm((N_ODD, Q_LORA, MLA_HEADS * (QK_NOPE + QK_ROPE)), Q_LORA ** -0.5),
        'mla_kv_norm_g': gain((N_ODD, KV_LORA)),
        'mla_w_ukv': nrm((N_ODD, KV_LORA, MLA_HEADS * (QK_NOPE + V_DIM)), KV_LORA ** -0.5),
        'mla_w_o': nrm((N_ODD, MLA_HEADS * V_DIM, D_MODEL), (MLA_HEADS * V_DIM) ** -0.5),
        'ffn_w_up': nrm((DEPTH, D_MODEL, 2 * D_FF), D_MODEL ** -0.5),
        'ffn_conv_w': nrm((DEPTH, FFN_CONV, D_FF), FFN_CONV ** -0.5),
        'ffn_conv_b': nrm((DEPTH, D_FF), 0.02),
        'ffn_w_down': nrm((DEPTH, D_FF, D_MODEL), D_FF ** -0.5),
        'final_norm_g': gain((D_MODEL,)),
    }


def reference(x, c, ctx, c_ctx, norm1_g, norm2_g, w_ada, b_ada, ab_w_in, ab_b_in, a_ln_g, a_ln_b, a_w_s, a_b_s, b_conv_w, b_conv_b, b_ln_g, b_ln_b, ab_w_out, mla_w_in, mla_q_norm_g, mla_w_uq, mla_kv_norm_g, mla_w_ukv, mla_w_o, ffn_w_up, ffn_conv_w, ffn_conv_b, ffn_w_down, final_norm_g):
    L = x.shape[1]
    cos, sin = axial_rope(L)
    xl, xc = x, ctx
    for i in range(DEPTH):
        last = i == DEPTH - 1
        even = i % 2 == 0
        j = i // 2
        ctx_in = (not last) or (not even)
        ctx_update = not last
        sh1, sc1, g1, sh2, sc2, g2 = [m[:, None, :] for m in adaln(c, w_ada[i], b_ada[i])]
        if ctx_in:
            csh1, csc1, cg1, csh2, csc2, cg2 = adaln(c_ctx, w_ada[i], b_ada[i])
        hl = modulate(rmsnorm(xl, norm1_g[i]), sh1, sc1)
        if even:
            ab_args = (ab_w_in[j], ab_b_in[j], a_ln_g[j], a_ln_b[j], a_w_s[j], a_b_s[j], b_conv_w[j], b_conv_b[j], b_ln_g[j], b_ln_b[j], ab_w_out[j])
            yl = ab_mixer(hl, *ab_args)
            if ctx_update:
                hc = modulate(rmsnorm(xc, norm1_g[i]), csh1, csc1)
                xc = xc + cg1 * ab_mixer(hc, *ab_args)
        else:
            hc = modulate(rmsnorm(xc, norm1_g[i]), csh1, csc1)
            yl, yc = mla_mixer(hl, hc, mla_w_in[j], mla_q_norm_g[j], mla_w_uq[j], mla_kv_norm_g[j], mla_w_ukv[j], mla_w_o[j], cos, sin, ctx_update)
            if ctx_update:
                xc = xc + cg1 * yc
        xl = xl + g1 * yl
        xl = xl + g2 * conv_ffn(modulate(rmsnorm(xl, norm2_g[i]), sh2, sc2), ffn_w_up[i], ffn_conv_w[i], ffn_conv_b[i], ffn_w_down[i])
        if ctx_update:
            xc = xc + cg2 * conv_ffn(modulate(rmsnorm(xc, norm2_g[i]), csh2, csc2), ffn_w_up[i], ffn_conv_w[i], ffn_conv_b[i], ffn_w_down[i])
    return rmsnorm(xl, final_norm_g)
```

```python
import numpy as np
import concourse.bass as bass
import concourse.mybir as mybir
from concourse.bass_utils import run_bass_kernel_spmd

F32 = mybir.dt.float32
BF16 = mybir.dt.bfloat16
AF = mybir.ActivationFunctionType
ALU = mybir.AluOpType

D = 2048
KT = 16
L = 4096
NL = 1024
NCX = 128
NCEN = NL + NCX
HB = 15
NM = (NL + 2 * HB) + (NCX + 2 * HB)
LOFF = HB
COFF = NL + 2 * HB + HB
DFF = 5632
FCH = 256
NFCH = DFF // FCH
SLOT = 4096
NRING = 4
EPS = 1e-6
NKEY = 4096 + 256
NKT = NKEY // 128
ATT_SCALE = 192.0 ** -0.5
GELU_C = 1.5957691216057308


class Buf:
    __slots__ = ("w", "r", "name")

    def __init__(self, name=""):
        self.w = None
        self.r = []
        self.name = name


class Q:
    def __init__(self, name, key, sem, is_pe=False):
        self.name, self.key, self.sem, self.is_pe = name, key, sem, is_pe
        self.cnt = 0
        self.ops = []
        self.seen = {}


class Ctx:
    def __init__(self, nc):
        self.nc = nc
        self.sems = []
        self.queues = {}
        self.dma_sems = []
        self.dma_rr = 0
        self.n_inst = 0

    def new_sem(self, name):
        s = self.nc.alloc_semaphore(name)
        self.sems.append(s)
        return len(self.sems) - 1

    def add_queue(self, name, is_pe=False):
        key = self.new_sem("q_" + name)
        q = Q(name, key, self.sems[key], is_pe)
        self.queues[name] = q
        return q

    def add_dma_sems(self, n):
        for i in range(n):
            self.dma_sems.append([self.new_sem(f"dma{i}"), 0])

    def _waits(self, q, deps, skip_same):
        for sk, v in deps.items():
            if skip_same and sk == q.key:
                continue
            if q.seen.get(sk, 0) >= v:
                continue
            q.seen[sk] = v
            q.ops.append(("wait", sk, v))

    @staticmethod
    def _collect(reads, writes):
        deps = {}

        def add(ev):
            if ev is not None and deps.get(ev[0], 0) < ev[1]:
                deps[ev[0]] = ev[1]
        for b in reads:
            add(b.w)
        for b in writes:
            add(b.w)
            for r in b.r:
                add(r)
        return deps

    def emit(self, q, fn, reads=(), writes=()):
        deps = self._collect(reads, writes)
        self._waits(q, deps, skip_same=q.is_pe)
        q.cnt += 1
        ev = (q.key, q.cnt)
        q.ops.append(("op", fn))
        self.n_inst += 1
        for b in writes:
            b.w = ev
            b.r = []
        for b in reads:
            if b not in writes:
                b.r.append(ev)

    def dma(self, q, out_ap, in_ap, reads=(), writes=()):
        deps = self._collect(reads, writes)
        ds = self.dma_sems[self.dma_rr]
        self.dma_rr = (self.dma_rr + 1) % len(self.dma_sems)
        if ds[1] > 0 and deps.get(ds[0], 0) < ds[1]:
            deps[ds[0]] = ds[1]
        self._waits(q, deps, skip_same=False)
        ds[1] += 16
        ev = (ds[0], ds[1])
        sem = self.sems[ds[0]]
        q.ops.append(("dma", out_ap, in_ap, sem))
        self.n_inst += 1
        for b in writes:
            b.w = ev
            b.r = []
        for b in reads:
            if b not in writes:
                b.r.append(ev)
        return ev

    def barrier(self, names):
        tot = {}
        for q in self.queues.values():
            if q.cnt:
                tot[q.key] = q.cnt
        for k, c in self.dma_sems:
            if c:
                tot[k] = c
        for n in names:
            q = self.queues[n]
            self._waits(q, dict(tot), skip_same=True)

    def replay(self, q, eng):
        sems = self.sems
        for op in q.ops:
            if op[0] == "wait":
                eng.wait_ge(sems[op[1]], op[2])
            elif op[0] == "op":
                op[1](eng).then_inc(q.sem, 1)
            elif op[0] == "cc":
                op[1](eng).then_inc(self.cc_sem)
            else:
                eng.dma_start(out=op[1], in_=op[2]).then_inc(op[3], 16)


class Arena:
    def __init__(self, nc, base, limit):
        self.nc, self.base, self.limit = nc, base, limit
        self.top = base
        self.n = 0

    def alloc(self, shape, dtype, name=None):
        esz = 2 if dtype == BF16 else 4
        per = esz
        for s in shape[1:]:
            per *= s
        off = (self.top + 31) // 32 * 32
        assert off + per <= self.limit, f"arena overflow {name}: {off}+{per} > {self.limit}"
        self.top = off + per
        self.n += 1
        return self.nc.alloc_sbuf_tensor_at(f"{name or 't'}_{self.n}_{off}", list(shape), dtype, offset=off)

    def mark(self):
        return self.top

    def reset(self, m):
        self.top = m


MW_BLOCKS = [(0, 512, 0, 6), (512, 512, 0, 6), (1024, 128, 3, 6)]


def slot_plan():
    plan = []
    plan += [("ada0", SLOT)] * 48
    plan += [("win", SLOT)] * 16
    plan += [("wout", SLOT)] * 8
    plan += [("ada1", SLOT)] * 48
    plan += [("f0" + k, SLOT) for (k, _) in ffn_order()]
    for (_, _, s0, s1) in MW_BLOCKS:
        plan += [("mwin", SLOT)] * (s1 - s0)
    for h in range(16):
        plan += [("hA", 2560), ("hB", 2048)]
    plan += [("f1" + k, SLOT) for (k, _) in ffn_order()]
    return plan


def ffn_order():
    o = [("g", 0), ("u", 0)]
    for f in range(1, NFCH):
        o += [("g", f), ("u", f), ("d", f - 1)]
    o += [("d", NFCH - 1)]
    return o


def col_blocks(n, nb=None):
    if nb is None:
        nb = (n + 511) // 512
    base = (n + nb - 1) // nb
    out = []
    c = 0
    while c < n:
        w = min(base, n - c)
        out.append((c, w))
        c += w
    return out


TAG_N = {"hA": 2560, "hB": 2048}


def build_nc(debug=(), stop_after=None, order=None):
    nc = bass.Bass("TRN2", target_bir_lowering=False)
    plan = slot_plan()
    NS = len(plan)
    record = order is None
    plan_rec = [] if record else list(order)
    if stop_after in ("p1", "p2"):
        NS = 48 + 16 + 8 + 48 + 3 * NFCH
    if stop_after == "p3":
        NS = 48 + 16 + 8 + 48 + 3 * NFCH + 15 + 3

    def din(name, shape, dt=F32):
        return nc.dram_tensor(name, list(shape), dt, kind="ExternalInput")

    wstream = din("wstream", [NS, 128, SLOT])
    xT = din("xT", [D, NM])
    cmat = din("cmat", [128, KT, 2])
    bada = din("bada", [128, 2, 96])
    ng = din("ng", [128, 5, KT])
    bin_fm = din("bin_fm", [128, 32])
    binv = din("binv", [128, 1024])
    alng = din("alng", [128, 1024])
    alnb = din("alnb", [128, 1024])
    wst = din("wst", [128, 8, 128])
    bsb = din("bsb", [128, 8, 128])
    bcw = din("bcw", [128, 8, 31])
    bsm = din("bsm", [128, 8, 3])
    maskb = din("maskb", [128, 4, HB])
    fcw = din("fcw", [128, 2, 44, 4])
    qkg = din("qkg", [128, 10])
    ropet = din("ropet", [64, 2, NL])
    ohs = din("ohs", [128, 4, 8])
    bsel = din("bsel", [128, 2])
    outT = nc.dram_tensor("outT", [D, NL], F32, kind="ExternalOutput")

    cc_in = [nc.dram_tensor(f"cc_in{i}", [128, 64], BF16) for i in range(2)]
    cc_out = [nc.dram_tensor(f"cc_out{i}", [8 * 128, 64], BF16) for i in range(2)]
    cc_in_b = [Buf(), Buf()]
    cc_out_b = [Buf(), Buf()]
    kv_in = nc.dram_tensor("kv_in", [640, NCEN], BF16)
    kv_out = nc.dram_tensor("kv_out", [8 * 640, NCEN], BF16)
    kv_in_b, kv_out_b = Buf(), Buf()

    C = Ctx(nc)
    PE = C.add_queue("pe", is_pe=True)
    ACT = C.add_queue("act")
    DVE = C.add_queue("dve")
    POOL = C.add_queue("pool")
    SP = C.add_queue("sp")
    C.add_dma_sems(24)
    cc_sem = nc.alloc_semaphore("cc_sem")
    cc_key = len(C.sems)
    C.sems.append(cc_sem)
    C.cc_sem = cc_sem
    cc_count = [0]

    SB_LIMIT = int(nc.SBUF_PARTITION_SIZE_BYTES)
    SB_BASE = (SB_LIMIT - int(nc.sbuf_bytes_remaining) + 63) // 64 * 64
    ar = Arena(nc, SB_BASE, SB_LIMIT)

    ring_t = [ar.alloc([128, SLOT], BF16, "ring") for _ in range(NRING)]
    ring_b = [Buf(f"ring{i}") for i in range(NRING)]
    ones = ar.alloc([128, 128], BF16, "ones")
    ones_b = Buf("ones")
    onesm = {}
    for nm in ("m2048", "m1024", "m768", "m512"):
        onesm[nm] = (ar.alloc([128, 128], BF16, nm), Buf(nm))
    MOD = ar.alloc([128, 2, 2, 96], F32, "mod")
    MOD_b = Buf("mod")
    AB = ar.alloc([128, 2, 2, 2, KT], F32, "ab")
    AB_b = Buf("ab")
    NG = ar.alloc([128, 5, KT], F32, "ng")
    NG_b = Buf("ng")
    FCW = ar.alloc([128, 2, 44, 4], F32, "fcw")
    FCW_b = Buf("fcw")
    OHS = ar.alloc([128, 4, 8], F32, "ohs")
    OHS_b = Buf("ohs")
    QKG = ar.alloc([128, 10], F32, "qkg")
    QKG_b = Buf("qkg")
    BSEL = ar.alloc([128, 2], F32, "bsel")
    BSEL_b = Buf("bsel")
    CM = ar.alloc([128, KT, 2], F32, "cm")
    CM_b = Buf("cm")
    BADA = ar.alloc([128, 2, 96], F32, "bada")
    BADA_b = Buf("bada")
    SC = ar.alloc([128, KT, 2], BF16, "sc")
    SC_b = Buf("sc")
    EPSB = ar.alloc([128, 1], F32, "epsb")
    EPSB_b = Buf("epsb")
    xl_off = (ar.top + 31) // 32 * 32
    XL = ar.alloc([128, KT, NCEN], F32, "xl")
    XL_b = [[Buf(f"xl{i}_{j}") for j in range(3)] for i in range(KT)]
    XL_all = [b for row in XL_b for b in row]
    xl_end = ar.top
    ph_base = ar.top

    ps_t = [nc.alloc_psum_tensor(f"ps{i}", [128, 512], F32) for i in range(8)]
    ps_b = [Buf(f"ps{i}") for i in range(8)]
    ps_rr = [0]
    PS_POOL = [8]

    def next_ps():
        i = ps_rr[0] % PS_POOL[0]
        ps_rr[0] = (i + 1) % PS_POOL[0]
        return ps_t[i], ps_b[i]

    def mm(out, lhsT, rhs, start, stop, reads, writes):
        C.emit(PE, lambda e: e.matmul(out, lhsT=lhsT, rhs=rhs, start=start, stop=stop), reads, writes)

    def act(out, in_, func, reads, writes, bias=0.0, scale=1.0):
        C.emit(ACT, lambda e: e.activation(out=out, in_=in_, func=func, bias=bias, scale=scale), reads, writes)

    def tt(out, in0, in1, op, reads, writes, q=None):
        C.emit(q or DVE, lambda e: e.tensor_tensor(out=out, in0=in0, in1=in1, op=op), reads, writes)

    def ts(out, in0, s1, s2, op0, op1, reads, writes, q=None):
        if s2 is None:
            C.emit(q or DVE, lambda e: e.tensor_scalar(out=out, in0=in0, scalar1=s1, scalar2=None, op0=op0),
                   reads, writes)
        else:
            C.emit(q or DVE, lambda e: e.tensor_scalar(out=out, in0=in0, scalar1=s1, scalar2=s2, op0=op0, op1=op1),
                   reads, writes)

    def stt(out, in0, scalar, in1, op0, op1, reads, writes, q=None):
        C.emit(q or DVE, lambda e: e.scalar_tensor_tensor(out=out, in0=in0, scalar=scalar, in1=in1, op0=op0, op1=op1),
               reads, writes)

    def cp(out, in_, reads, writes, q=None):
        C.emit(q or DVE, lambda e: e.tensor_copy(out=out, in_=in_), reads, writes)

    def recip(out, in_, reads, writes):
        C.emit(DVE, lambda e: e.reciprocal(out=out, in_=in_), reads, writes)

    def rsum(out, in_, reads, writes):
        C.emit(DVE, lambda e: e.reduce_sum(out=out, in_=in_, axis=mybir.AxisListType.X), reads, writes)

    def memset(out, val, writes, q=None):
        C.emit(q or DVE, lambda e: e.memset(out, val), (), writes)

    def load(out, in_, writes, reads=(), q=None):
        C.dma(q or SP, out, in_, reads, writes)

    def collective(groups, in_t, in_b, out_t, out_b):
        deps = C._collect([in_b], [out_b])
        C._waits(POOL, deps, skip_same=False)
        cc_count[0] += 1
        ev = (cc_key, cc_count[0])
        POOL.ops.append(("cc", lambda e: e.collective_compute(
            "AllGather", ALU.bypass, replica_groups=groups, ins=[in_t.ap().opt()], outs=[out_t.ap().opt()])))
        out_b.w = ev
        out_b.r = []
        in_b.r.append(ev)

    wi = [0]
    wq = [0]

    def prefetch(k):
        if record and k > 1:
            return
        while wq[0] < min(wi[0] + k, len(plan_rec)):
            i = wq[0]
            n = TAG_N.get(plan_rec[i], SLOT)
            r = i % NRING
            C.dma(POOL, ring_t[r][:, 0:n], wstream[i, :, 0:n], (), [ring_b[r]])
            wq[0] += 1

    def next_w(tag):
        i = wi[0]
        if record:
            plan_rec.append(tag)
        assert plan_rec[i] == tag, (i, plan_rec[i], tag)
        prefetch(1)
        wi[0] += 1
        r = i % NRING
        return ring_t[r], ring_b[r]

    def dump(name, ap, bufs, shape, dt=F32):
        if name not in debug:
            return
        o = nc.dram_tensor("dbg_" + name, list(shape), dt, kind="ExternalOutput")
        C.dma(SP, o.ap(), ap, bufs, [Buf()])

    def finish():
        C.barrier(["sp"])

    memset(ones[:, :], 1.0, [ones_b])
    for nm, val in (("m2048", 1.0 / 2048), ("m1024", 1.0 / 1024), ("m768", 1.0 / 768), ("m512", 1.0 / 512)):
        memset(onesm[nm][0][:, :], val, [onesm[nm][1]])
    memset(EPSB[:, :], EPS, [EPSB_b])
    load(NG[:, :, :], ng.ap(), [NG_b])
    load(FCW[:, :, :, :], fcw.ap(), [FCW_b])
    load(OHS[:, :, :], ohs.ap(), [OHS_b])
    load(QKG[:, :], qkg.ap(), [QKG_b])
    load(BSEL[:, :], bsel.ap(), [BSEL_b])
    load(CM[:, :, :], cmat.ap(), [CM_b])
    load(BADA[:, :, :], bada.ap(), [BADA_b])
    act(SC[:, :, :], CM[:, :, :], AF.Silu, [CM_b], [SC_b])

    ada_s = [0, 0]

    def ada_steps(l, k):
        pst, psb = ps_t[7], ps_b[7]
        for _ in range(k):
            sidx = ada_s[l]
            if sidx >= 48:
                return
            ada_s[l] += 1
            W, Wb = next_w(f"ada{l}")
            Wv = W[:, :].rearrange("p (k c) -> p k c", k=KT)
            for m in range(2):
                mt = sidx * 2 + m
                for kt in range(KT):
                    mm(pst[:, mt * 2:mt * 2 + 2], Wv[:, kt, m * 128:(m + 1) * 128], SC[:, kt, :],
                       kt == 0, kt == KT - 1, [Wb, SC_b], [psb])

    def ada_finalize(l, j0, j1):
        pst, psb = ps_t[7], ps_b[7]
        psv = pst[:, 0:192].rearrange("p (m r) -> p m r", r=2)
        for r in range(2):
            tt(MOD[:, l, r, j0 * 16:j1 * 16], psv[:, j0 * 16:j1 * 16, r], BADA[:, l, j0 * 16:j1 * 16], ALU.add,
               [psb, BADA_b], [MOD_b])

    def ada_ab(l, which):
        for r in range(2):
            if which == 0:
                stt(AB[:, l, r, 0, :], MOD[:, l, r, 16:32], 1.0, NG[:, l, :], ALU.add, ALU.mult, [MOD_b, NG_b], [AB_b])
            else:
                stt(AB[:, l, r, 1, :], MOD[:, l, r, 64:80], 1.0, NG[:, 2 + l, :], ALU.add, ALU.mult, [MOD_b, NG_b],
                    [AB_b])

    def modv(l, r, j, kt):
        return MOD[:, l, r, j * 16 + kt:j * 16 + kt + 1]

    def rms_rstd(x_aps, x_bufs, n, onesname, SQ, SQ_b, rstd_ap, rstd_b):
        pst, psb = next_ps()
        nk = len(x_aps)
        om, omb = onesm[onesname]
        for k in range(nk):
            j = k % 2
            act(SQ[:, j, 0:n], x_aps[k], AF.Square, [x_bufs[k]], [SQ_b[j]])
            mm(pst[:, 0:n], om[:, :], SQ[:, j, 0:n], k == 0, k == nk - 1, [omb, SQ_b[j]], [psb])
        act(rstd_ap, pst[:, 0:n], AF.Sqrt, [psb, EPSB_b], [rstd_b], bias=EPSB[:, 0:1])
        recip(rstd_ap, rstd_ap, [rstd_b], [rstd_b])

    def gelu_from(X, Xb, T, Tb, out, outb):
        act(T, X, AF.Square, [Xb], [Tb])
        ts(T, T, 0.044715, 1.0, ALU.mult, ALU.add, [Tb], [Tb])
        tt(T, T, X, ALU.mult, [Tb, Xb], [Tb])
        act(T, T, AF.Sigmoid, [Tb], [Tb], scale=GELU_C)
        tt(out, T, X, ALU.mult, [Tb, Xb], [outb])

    PS_POOL[0] = 7
    ada_steps(0, 16)
    ada_finalize(0, 0, 2)
    ada_ab(0, 0)

    ph = Arena(nc, ph_base, SB_LIMIT)
    ph.top = ph_base
    YA = ph.alloc([128, 8, NCEN], BF16, "ya")
    YA_b = [Buf(f"ya{i}") for i in range(8)]
    YB = ph.alloc([128, 8, NCEN], BF16, "yb")
    YB_b = [Buf(f"yb{i}") for i in range(8)]
    BINF = ph.alloc([128, 32], F32, "binf")
    BINF_b = Buf()
    BCW = ph.alloc([128, 8, 31], F32, "bcw")
    BCW_b = Buf()
    BSM = ph.alloc([128, 8, 3], F32, "bsm")
    BSM_b = Buf()
    MASKB = ph.alloc([128, 4, HB], F32, "maskb")
    MASKB_b = Buf()
    WST = ph.alloc([128, 8, 128], BF16, "wst")
    WST_b = Buf()
    BINV = ph.alloc([128, 1024], F32, "binv")
    BINV_b = Buf()
    ALNG = ph.alloc([128, 1024], F32, "alng")
    ALNG_b = Buf()
    ALNB = ph.alloc([128, 1024], F32, "alnb")
    ALNB_b = Buf()
    BSB = ph.alloc([128, 8, 128], F32, "bsb")
    BSB_b = Buf()
    SQ = ph.alloc([128, 2, 512], BF16, "sq")
    SQ_b = [Buf(), Buf()]
    RSTD = ph.alloc([128, 512], F32, "rstd")
    RSTD_b = Buf()
    TMP = ph.alloc([128, 2, 512], F32, "tmp")
    TMP_b = [Buf(), Buf()]
    TMP2 = ph.alloc([128, 2, 512], F32, "tmp2")
    TMP2_b = [Buf(), Buf()]
    XB = ph.alloc([128, 2, KT, 64], F32, "xb")
    XB_b = [Buf(), Buf()]
    STAT = ph.alloc([128, 16], F32, "stat")
    STAT_b = Buf()
    ACC = ph.alloc([128, 2, NCEN], F32, "acc")
    ph_acc = [ACC[:, 0, :], ACC[:, 1, :]]
    ph_acc_b = [Buf(), Buf()]
    load(BINF[:, :], bin_fm.ap(), [BINF_b])
    load(BCW[:, :, :], bcw.ap(), [BCW_b])
    load(BSM[:, :, :], bsm.ap(), [BSM_b])
    load(MASKB[:, :, :], maskb.ap(), [MASKB_b])
    C.dma(POOL, WST[:, :, :], wst.ap(), (), [WST_b])
    load(BINV[:, :], binv.ap(), [BINV_b])
    load(ALNG[:, :], alng.ap(), [ALNG_b])
    load(ALNB[:, :], alnb.ap(), [ALNB_b])
    load(BSB[:, :, :], bsb.ap(), [BSB_b])
    xa = Arena(nc, xl_off, xl_end)
    HN = xa.alloc([128, KT, NM], BF16, "hn")
    HN_b = [Buf(f"hn{i}") for i in range(KT)]
    VN = xa.alloc([128, 9, 1024], BF16, "vn")
    VN_b = [Buf(f"vn{i}") for i in range(9)]
    HG = xa.alloc([128, 2, NM], BF16, "hg")
    HG_b = [Buf(), Buf()]
    ZA = xa.alloc([128, 2, NM], BF16, "za")
    ZA_b = [Buf(), Buf()]

    xTv = xT.ap().rearrange("(k p) c -> p k c", p=128)
    blk = [(c, min(64, NM - c)) for c in range(0, NM, 64)]
    for bi, (c0, n) in enumerate(blk):
        j = bi % 2
        load(XB[:, j, :, 0:n], xTv[:, :, c0:c0 + n], [XB_b[j]])
        rms_rstd([XB[:, j, kt, 0:n] for kt in range(KT)], [XB_b[j]] * KT, n, "m2048", SQ, SQ_b,
                 RSTD[:, 0:n], RSTD_b)
        segs = []
        lat_end = NL + 2 * HB
        if c0 < lat_end:
            segs.append((0, min(n, lat_end - c0), 0))
        if c0 + n > lat_end:
            s0 = max(0, lat_end - c0)
            segs.append((s0, n, 1))
        for kt in range(KT):
            for (a, b, r) in segs:
                tj = kt % 2
                stt(TMP[:, tj, a:b], XB[:, j, kt, a:b], AB[:, 0, r, 0, kt:kt + 1], RSTD[:, a:b], ALU.mult, ALU.mult,
                    [XB_b[j], AB_b, RSTD_b], [TMP_b[tj]])
                act(HN[:, kt, c0 + a:c0 + b], TMP[:, tj, a:b], AF.Identity, [TMP_b[tj], MOD_b], [HN_b[kt]],
                    bias=modv(0, r, 0, kt))
    dump("hn", HN[:, :, :], HN_b, [128, KT, NM], BF16)

    CEN = [(LOFF, 512, 0, 0), (LOFF + 512, 512, 512, 0), (COFF, 128, 1024, 1)]
    MBLK = col_blocks(NM, 3)

    for s in range(4):
        W, Wb = next_w("win")
        ada_steps(0, 2)
        Wv = W[:, :].rearrange("p (k c) -> p k c", k=KT)
        for m in range(2):
            mt = s * 2 + m
            for (mc0, n, cc0, r) in CEN:
                pst, psb = next_ps()
                for kt in range(KT):
                    mm(pst[:, 0:n], Wv[:, kt, m * 128:(m + 1) * 128], HN[:, kt, mc0:mc0 + n], kt == 0, kt == KT - 1,
                       [Wb, HN_b[kt]], [psb])
                j = (mt + (cc0 // 512)) % 2
                act(TMP[:, j, 0:n], pst[:, 0:n], AF.Identity, [psb, BINF_b], [TMP_b[j]], bias=BINF[:, mt:mt + 1])
                gelu_from(TMP[:, j, 0:n], TMP_b[j], TMP2[:, j, 0:n], TMP2_b[j], YA[:, mt, cc0:cc0 + n], YA_b[mt])
    dump("u", YA[:, :, :], YA_b, [128, 8, NCEN], BF16)

    chunk_cols = [LOFF + 128 * c for c in range(8)] + [COFF]
    for s in range(4):
        W, Wb = next_w("win")
        ada_steps(0, 2)
        Wv = W[:, :].rearrange("p (k c) -> p k c", k=KT)
        for c in range(9):
            pst, psb = next_ps()
            mc0 = chunk_cols[c]
            for kt in range(KT):
                mm(pst[:, 0:256], HN[:, kt, mc0:mc0 + 128], Wv[:, kt, :], kt == 0, kt == KT - 1,
                   [Wb, HN_b[kt]], [psb])
            j = c % 2
            tt(TMP[:, j, 0:256], pst[:, 0:256], BINV[:, s * 256:(s + 1) * 256], ALU.add, [psb, BINV_b], [TMP_b[j]])
            gelu_from(TMP[:, j, 0:256], TMP_b[j], TMP2[:, j, 0:256], TMP2_b[j],
                      VN[:, c, s * 256:(s + 1) * 256], VN_b[c])
    TMPf = TMP[:, :, :].rearrange("p a b -> p (a b)")
    TMP2f = TMP2[:, :, :].rearrange("p a b -> p (a b)")
    for c in range(9):
        rsum(STAT[:, 0:1], VN[:, c, :], [VN_b[c]], [STAT_b])
        act(TMPf, VN[:, c, :], AF.Square, [VN_b[c]], TMP_b)
        rsum(STAT[:, 1:2], TMPf, TMP_b, [STAT_b])
        ts(STAT[:, 2:3], STAT[:, 0:1], 1.0 / 1024, None, ALU.mult, None, [STAT_b], [STAT_b])
        tt(STAT[:, 3:4], STAT[:, 2:3], STAT[:, 2:3], ALU.mult, [STAT_b], [STAT_b])
        stt(STAT[:, 4:5], STAT[:, 1:2], 1.0 / 1024, STAT[:, 3:4], ALU.mult, ALU.subtract, [STAT_b], [STAT_b])
        act(STAT[:, 5:6], STAT[:, 4:5], AF.Sqrt, [STAT_b, EPSB_b], [STAT_b], bias=EPSB[:, 0:1])
        recip(STAT[:, 6:7], STAT[:, 5:6], [STAT_b], [STAT_b])
        ts(TMPf, VN[:, c, :], STAT[:, 2:3], STAT[:, 6:7], ALU.subtract, ALU.mult, [VN_b[c], STAT_b], TMP_b)
        tt(TMPf, TMPf, ALNG[:, :], ALU.mult, TMP_b + [ALNG_b], TMP_b)
        tt(VN[:, c, :], TMPf, ALNB[:, :], ALU.add, TMP_b + [ALNB_b], [VN_b[c]])
    dump("vn", VN[:, :, :], VN_b, [128, 9, 1024], BF16)

    for h in range(8):
        for g0 in (0, 4, 8):
            cs = list(range(g0, min(g0 + 4, 9)))
            pst, psb = next_ps()
            for jj, c in enumerate(cs):
                mm(pst[:, jj * 128:(jj + 1) * 128], VN[:, c, h * 128:(h + 1) * 128], WST[:, h, :], True, True,
                   [VN_b[c], WST_b], [psb])
            for jj, c in enumerate(cs):
                tj = c % 2
                tt(TMP[:, tj, 0:128], pst[:, jj * 128:(jj + 1) * 128], BSB[:, h, :], ALU.add, [psb, BSB_b], [TMP_b[tj]])
                tt(YA[:, h, c * 128:(c + 1) * 128], TMP[:, tj, 0:128], YA[:, h, c * 128:(c + 1) * 128], ALU.mult,
                   [TMP_b[tj], YA_b[h]], [YA_b[h]])
    dump("ya", YA[:, :, :], YA_b, [128, 8, NCEN], BF16)

    for pr in range(4):
        Wa, Wab = next_w("win")
        ada_steps(0, 2)
        Wav = Wa[:, :].rearrange("p (k c) -> p k c", k=KT)
        for m in range(2):
            ct = pr * 2 + m
            for (c0, n) in MBLK:
                pst, psb = next_ps()
                for kt in range(KT):
                    mm(pst[:, 0:n], Wav[:, kt, m * 128:(m + 1) * 128], HN[:, kt, c0:c0 + n], kt == 0, kt == KT - 1,
                       [Wab, HN_b[kt]], [psb])
                act(ZA[:, m, c0:c0 + n], pst[:, 0:n], AF.Identity, [psb, BINF_b], [ZA_b[m]],
                    bias=BINF[:, 16 + ct:17 + ct])
        Wg, Wgb = next_w("win")
        ada_steps(0, 2)
        Wgv = Wg[:, :].rearrange("p (k c) -> p k c", k=KT)
        for m in range(2):
            ct = pr * 2 + m
            for bi, (c0, n) in enumerate(MBLK):
                pst, psb = next_ps()
                for kt in range(KT):
                    mm(pst[:, 0:n], Wgv[:, kt, m * 128:(m + 1) * 128], HN[:, kt, c0:c0 + n], kt == 0, kt == KT - 1,
                       [Wgb, HN_b[kt]], [psb])
                tj = bi % 2
                act(TMP[:, tj, 0:n], pst[:, 0:n], AF.Sigmoid, [psb, BINF_b], [TMP_b[tj]],
                    bias=BINF[:, 24 + ct:25 + ct])
                tt(HG[:, m, c0:c0 + n], TMP[:, tj, 0:n], ZA[:, m, c0:c0 + n], ALU.mult, [TMP_b[tj], ZA_b[m]], [HG_b[m]])
            for i, hc in enumerate((0, NL + HB, NL + 2 * HB, NL + 2 * HB + HB + NCX)):
                tt(HG[:, m, hc:hc + HB], HG[:, m, hc:hc + HB], MASKB[:, i, :], ALU.mult, [HG_b[m], MASKB_b], [HG_b[m]])
            acc = ph_acc[m % 2]
            accb = ph_acc_b[m % 2]
            segs = [(0, NL, 0), (NL + 2 * HB, NCX, NL)]
            for k in range(31):
                for (sb, n, cc0) in segs:
                    src = HG[:, m, sb + k:sb + k + n]
                    if k == 0:
                        ts(acc[:, cc0:cc0 + n], src, BCW[:, ct, 0:1], BSM[:, ct, 0:1], ALU.mult, ALU.add,
                           [HG_b[m], BCW_b, BSM_b], [accb])
                    elif k < 30:
                        stt(acc[:, cc0:cc0 + n], src, BCW[:, ct, k:k + 1], acc[:, cc0:cc0 + n], ALU.mult, ALU.add,
                            [HG_b[m], BCW_b, accb], [accb])
                    else:
                        stt(YB[:, ct, cc0:cc0 + n], src, BCW[:, ct, k:k + 1], acc[:, cc0:cc0 + n], ALU.mult, ALU.add,
                            [HG_b[m], BCW_b, accb], [YB_b[ct]])
    dump("convb", YB[:, :, :], YB_b, [128, 8, NCEN], BF16)

    CB3 = [(0, 512), (512, 512), (1024, 128)]
    om1024, om1024b = onesm["m1024"]
    for (c0, n) in CB3:
        pst, psb = next_ps()
        for ct in range(8):
            mm(pst[:, 0:n], om1024[:, :], YB[:, ct, c0:c0 + n], ct == 0, ct == 7, [om1024b, YB_b[ct]], [psb])
        cp(RSTD[:, 0:n], pst[:, 0:n], [psb], [RSTD_b])
        pst2, psb2 = next_ps()
        for ct in range(8):
            tt(YB[:, ct, c0:c0 + n], YB[:, ct, c0:c0 + n], RSTD[:, 0:n], ALU.subtract, [YB_b[ct], RSTD_b], [YB_b[ct]])
            j = ct % 2
            act(SQ[:, j, 0:n], YB[:, ct, c0:c0 + n], AF.Square, [YB_b[ct]], [SQ_b[j]])
            mm(pst2[:, 0:n], om1024[:, :], SQ[:, j, 0:n], ct == 0, ct == 7, [om1024b, SQ_b[j]], [psb2])
        act(RSTD[:, 0:n], pst2[:, 0:n], AF.Sqrt, [psb2, EPSB_b], [RSTD_b], bias=EPSB[:, 0:1])
        recip(RSTD[:, 0:n], RSTD[:, 0:n], [RSTD_b], [RSTD_b])
        for ct in range(8):
            j = ct % 2
            tt(TMP[:, j, 0:n], YB[:, ct, c0:c0 + n], RSTD[:, 0:n], ALU.mult, [YB_b[ct], RSTD_b], [TMP_b[j]])
            act(YB[:, ct, c0:c0 + n], TMP[:, j, 0:n], AF.Silu, [TMP_b[j], BSM_b], [YB_b[ct]],
                bias=BSM[:, ct, 2:3], scale=BSM[:, ct, 1:2])
    dump("yb", YB[:, :, :], YB_b, [128, 8, NCEN], BF16)

    ada_finalize(0, 2, 6)
    ada_ab(0, 1)
    C.barrier(["pe", "act", "dve", "sp"])
    XS = ACC
    XS_b = ph_acc_b
    for s in range(8):
        W, Wb = next_w("wout")
        Wv = W[:, :].rearrange("p (k c) -> p k c", k=KT)
        for m in range(2):
            mt = s * 2 + m
            j = mt % 2
            load(XS[:, j, 0:NL], xT[mt * 128:(mt + 1) * 128, LOFF:LOFF + NL], [XS_b[j]])
            load(XS[:, j, NL:NCEN], xT[mt * 128:(mt + 1) * 128, COFF:COFF + NCX], [XS_b[j]])
            for (c0, n) in CB3:
                r = 0 if c0 < NL else 1
                pst, psb = next_ps()
                for kt in range(KT):
                    src, srcb = (YA, YA_b) if kt < 8 else (YB, YB_b)
                    mm(pst[:, 0:n], Wv[:, kt, m * 128:(m + 1) * 128], src[:, kt % 8, c0:c0 + n], kt == 0, kt == KT - 1,
                       [Wb, srcb[kt % 8]], [psb])
                stt(XL[:, mt, c0:c0 + n], pst[:, 0:n], modv(0, r, 2, mt), XS[:, j, c0:c0 + n], ALU.mult, ALU.add,
                    [psb, MOD_b, XS_b[j]], [XL_b[mt][c0 // 512]])
    dump("xl1", XL[:, :, :], XL_all, [128, KT, NCEN])
    if stop_after == "p1":
        finish()
        return nc, C

    C.barrier(["pe", "act", "dve", "sp"])

    def ffn(l, with_ctx, xch):
        ph.top = ph_base
        NF = (NL + 2) + ((NCX + 2) if with_ctx else 0)
        segs = [(0, 512, 1, 0), (512, 512, 513, 0)]
        if with_ctx:
            segs.append((1024, 128, NL + 3, 1))
        H2 = ph.alloc([128, KT, NF], BF16, "h2")
        H2_b = [Buf() for _ in range(KT)]
        GT = ph.alloc([128, 2, NF], F32, "gt")
        GT_b = [[Buf() for _ in range(3)] for _ in range(2)]
        CT = ph.alloc([128, 2, NF], F32, "ct")
        CT_b = [Buf(), Buf()]
        UT = ph.alloc([128, 2, NF], BF16, "ut")
        UT_b = [[Buf() for _ in range(3)] for _ in range(2)]
        AT = ph.alloc([128, 2, 2, NF], BF16, "at")
        AT_b = [[Buf(), Buf()], [Buf(), Buf()]]
        SQ = ph.alloc([128, 2, 512], BF16, "sq")
        SQ_b = [Buf(), Buf()]
        RSTD = ph.alloc([128, 512], F32, "rstd")
        RSTD_b = Buf()
        TMP = ph.alloc([128, 2, 512], F32, "tmp")
        TMP_b = [Buf(), Buf()]
        SND = ph.alloc([128, 4, KT], BF16, "snd")
        SND_b = Buf()
        RCV = ph.alloc([128, 8, 64], BF16, "rcv")
        RCV_b = Buf()
        HAL = ph.alloc([128, 4, KT], F32, "hal")
        HAL_b = Buf()
        for (x0, n, h0, r) in segs:
            rms_rstd([XL[:, kt, x0:x0 + n] for kt in range(KT)], [XL_b[kt][x0 // 512] for kt in range(KT)], n, "m2048",
                     SQ, SQ_b, RSTD[:, 0:n], RSTD_b)
            for kt in range(KT):
                j = kt % 2
                stt(TMP[:, j, 0:n], XL[:, kt, x0:x0 + n], AB[:, l, r, 1, kt:kt + 1], RSTD[:, 0:n], ALU.mult, ALU.mult,
                    [XL_b[kt][x0 // 512], AB_b, RSTD_b], [TMP_b[j]])
                act(H2[:, kt, h0:h0 + n], TMP[:, j, 0:n], AF.Identity, [TMP_b[j], MOD_b], [H2_b[kt]],
                    bias=modv(l, r, 3, kt))
        memset(SND[:, :, :], 0.0, [SND_b])
        bcols = [1, NL] + ([NL + 3, NL + 2 + NCX] if with_ctx else [])
        for i, c in enumerate(bcols):
            cp(SND[:, i, :], H2[:, :, c], H2_b, [SND_b])
        load(cc_in[xch].ap(), SND[:, :, :].rearrange("p a b -> p (a b)"), [cc_in_b[xch]], [SND_b])
        prefetch(NRING - 1)
        collective([list(range(8))], cc_in[xch], cc_in_b[xch], cc_out[xch], cc_out_b[xch])
        load(RCV[:, :, :], cc_out[xch].ap().rearrange("(r p) c -> p r c", p=128), [RCV_b], [cc_out_b[xch]])
        hsrc = [1, 0, 3, 2]
        hcols = [0, NL + 1] + ([NL + 2, NL + 3 + NCX] if with_ctx else [])
        for i, hc in enumerate(hcols):
            sl = hsrc[i]
            for rr in range(8):
                src = RCV[:, rr, sl * 16:(sl + 1) * 16]
                if rr == 0:
                    ts(HAL[:, i, :], src, OHS[:, i, 0:1], None, ALU.mult, None, [RCV_b, OHS_b], [HAL_b])
                else:
                    stt(HAL[:, i, :], src, OHS[:, i, rr:rr + 1], HAL[:, i, :], ALU.mult, ALU.add,
                        [RCV_b, OHS_b, HAL_b], [HAL_b])
            cp(H2[:, :, hc], HAL[:, i, :], [HAL_b], H2_b)
        dump(f"h2_{l}", H2[:, :, :], H2_b, [128, KT, NF], BF16)
        FB = col_blocks(NF, 3)
        def up(f):
            Wg, Wgb = next_w(f"f{l}g")
            Wgv = Wg[:, :].rearrange("p (k c) -> p k c", k=KT)
            for ft in range(2):
                for bi, (c0, n) in enumerate(FB):
                    pst, psb = next_ps()
                    for kt in range(KT):
                        mm(pst[:, 0:n], Wgv[:, kt, ft * 128:(ft + 1) * 128], H2[:, kt, c0:c0 + n], kt == 0, kt == KT - 1,
                           [Wgb, H2_b[kt]], [psb])
                    act(GT[:, ft, c0:c0 + n], pst[:, 0:n], AF.Copy, [psb], [GT_b[ft][bi]])
            Wu, Wub = next_w(f"f{l}u")
            Wuv = Wu[:, :].rearrange("p (k c) -> p k c", k=KT)
            for ft in range(2):
                for bi, (c0, n) in enumerate(FB):
                    pst, psb = next_ps()
                    for kt in range(KT):
                        mm(pst[:, 0:n], Wuv[:, kt, ft * 128:(ft + 1) * 128], H2[:, kt, c0:c0 + n], kt == 0, kt == KT - 1,
                           [Wub, H2_b[kt]], [psb])
                    act(UT[:, ft, c0:c0 + n], pst[:, 0:n], AF.Copy, [psb], [UT_b[ft][bi]])
            pa = f % 2
            for ft in range(2):
                fi = f * 2 + ft
                n = NF - 2
                ts(CT[:, ft, 1:1 + n], GT[:, ft, 0:n], FCW[:, l, fi, 0:1], FCW[:, l, fi, 3:4], ALU.mult, ALU.add,
                   GT_b[ft] + [FCW_b], [CT_b[ft]])
                stt(CT[:, ft, 1:1 + n], GT[:, ft, 1:1 + n], FCW[:, l, fi, 1:2], CT[:, ft, 1:1 + n], ALU.mult, ALU.add,
                    GT_b[ft] + [FCW_b, CT_b[ft]], [CT_b[ft]])
                stt(CT[:, ft, 1:1 + n], GT[:, ft, 2:2 + n], FCW[:, l, fi, 2:3], CT[:, ft, 1:1 + n], ALU.mult, ALU.add,
                    GT_b[ft] + [FCW_b, CT_b[ft]], [CT_b[ft]])
                act(CT[:, ft, 1:1 + n], CT[:, ft, 1:1 + n], AF.Silu, [CT_b[ft]], [CT_b[ft]])
                tt(AT[:, pa, ft, 1:1 + n], CT[:, ft, 1:1 + n], UT[:, ft, 1:1 + n], ALU.mult, [CT_b[ft]] + UT_b[ft],
                   [AT_b[pa][ft]])

        def down(f):
            pa = f % 2
            Wd, Wdb = next_w(f"f{l}d")
            Wdv = Wd[:, :].rearrange("p (k c) -> p k c", k=2)
            for mt in range(KT):
                for (x0, n, h0, r) in segs:
                    pst, psb = next_ps()
                    for ft in range(2):
                        mm(pst[:, 0:n], Wdv[:, ft, mt * 128:(mt + 1) * 128], AT[:, pa, ft, h0:h0 + n], ft == 0, ft == 1,
                           [Wdb, AT_b[pa][ft]], [psb])
                    stt(XL[:, mt, x0:x0 + n], pst[:, 0:n], modv(l, r, 5, mt), XL[:, mt, x0:x0 + n], ALU.mult, ALU.add,
                        [psb, MOD_b, XL_b[mt][x0 // 512]], [XL_b[mt][x0 // 512]])

        for f in range(NFCH):
            up(f)
            if l == 0:
                ada_steps(1, 3 if f < 4 else 2)
            if f >= 1:
                down(f - 1)
        down(NFCH - 1)

    ffn(0, True, 0)
    ada_finalize(1, 0, 6)
    ada_ab(1, 0)
    ada_ab(1, 1)
    dump("xl2", XL[:, :, :], XL_all, [128, KT, NCEN])
    if stop_after == "p2":
        finish()
        return nc, C
    C.barrier(["pe", "act", "dve", "sp"])

    ph.top = ph_base
    CQN = ph.alloc([128, 6, NL], BF16, "cqn")
    CQN_b = [Buf() for _ in range(6)]
    ROPE = ph.alloc([128, 2, NL], F32, "rope")
    ROPE_b = Buf()
    load(ROPE[0:64, :, :], ropet.ap(), [ROPE_b])
    p34_base = ph.mark()
    CKVN = ph.alloc([128, 4, NCEN], BF16, "ckvn")
    CKVN_b = [Buf() for _ in range(4)]
    KPRO = ph.alloc([128, NCEN], BF16, "kpro")
    KPRO_b = Buf()
    H1 = ph.alloc([128, KT, 512], BF16, "h1")
    H1_b = [Buf() for _ in range(KT)]
    CQR = ph.alloc([128, 6, 512], F32, "cqr")
    CQR_b = [Buf() for _ in range(6)]
    CKR = ph.alloc([128, 4, 512], F32, "ckr")
    CKR_b = [Buf() for _ in range(4)]
    KP = ph.alloc([128, 2, 512], F32, "kp")
    KP_b = [Buf(), Buf()]
    SQ = ph.alloc([128, 2, 512], BF16, "sq")
    SQ_b = [Buf(), Buf()]
    RSTD = ph.alloc([128, 512], F32, "rstd")
    RSTD_b = Buf()
    TMP = ph.alloc([128, 2, 512], F32, "tmp")
    TMP_b = [Buf(), Buf()]
    for (x0, n, s0, s1) in MW_BLOCKS:
        r = 0 if x0 < NL else 1
        rms_rstd([XL[:, kt, x0:x0 + n] for kt in range(KT)], [XL_b[kt][x0 // 512] for kt in range(KT)], n, "m2048",
                 SQ, SQ_b, RSTD[:, 0:n], RSTD_b)
        for kt in range(KT):
            j = kt % 2
            stt(TMP[:, j, 0:n], XL[:, kt, x0:x0 + n], AB[:, 1, r, 0, kt:kt + 1], RSTD[:, 0:n], ALU.mult, ALU.mult,
                [XL_b[kt][x0 // 512], AB_b, RSTD_b], [TMP_b[j]])
            act(H1[:, kt, 0:n], TMP[:, j, 0:n], AF.Identity, [TMP_b[j], MOD_b], [H1_b[kt]], bias=modv(1, r, 0, kt))
        for s in range(s0, s1):
            W, Wb = next_w("mwin")
            Wv = W[:, :].rearrange("p (k c) -> p k c", k=KT)
            for m in range(2):
                g = s * 2 + m
                if g == 11:
                    continue
                if g < 10:
                    pst, psb = next_ps()
                    for kt in range(KT):
                        mm(pst[:, 0:n], Wv[:, kt, m * 128:(m + 1) * 128], H1[:, kt, 0:n], kt == 0, kt == KT - 1,
                           [Wb, H1_b[kt]], [psb])
                    if g < 6:
                        act(CQR[:, g, 0:n], pst[:, 0:n], AF.Copy, [psb], [CQR_b[g]])
                    else:
                        act(CKR[:, g - 6, 0:n], pst[:, 0:n], AF.Copy, [psb], [CKR_b[g - 6]])
                else:
                    for half in range(2):
                        pst, psb = next_ps()
                        for kt in range(KT):
                            mm(pst[0:64, 0:n], Wv[:, kt, half * 64:(half + 1) * 64], H1[:, kt, 0:n], kt == 0,
                               kt == KT - 1, [Wb, H1_b[kt]], [psb])
                        act(KP[0:64, half, 0:n], pst[0:64, 0:n], AF.Copy, [psb], [KP_b[half]])
        if r == 0:
            rms_rstd([CQR[:, k, 0:n] for k in range(6)], CQR_b, n, "m768", SQ, SQ_b, RSTD[:, 0:n], RSTD_b)
            for k in range(6):
                stt(CQN[:, k, x0:x0 + n], CQR[:, k, 0:n], QKG[:, k:k + 1], RSTD[:, 0:n], ALU.mult, ALU.mult,
                    [CQR_b[k], QKG_b, RSTD_b], [CQN_b[k]])
        rms_rstd([CKR[:, k, 0:n] for k in range(4)], CKR_b, n, "m512", SQ, SQ_b, RSTD[:, 0:n], RSTD_b)
        for k in range(4):
            stt(CKVN[:, k, x0:x0 + n], CKR[:, k, 0:n], QKG[:, 6 + k:7 + k], RSTD[:, 0:n], ALU.mult, ALU.mult,
                [CKR_b[k], QKG_b, RSTD_b], [CKVN_b[k]])
        if r == 0:
            tt(TMP[0:64, 0, 0:n], KP[0:64, 0, 0:n], ROPE[0:64, 0, x0:x0 + n], ALU.mult, [KP_b[0], ROPE_b], [TMP_b[0]])
            tt(TMP[0:64, 1, 0:n], KP[0:64, 1, 0:n], ROPE[0:64, 1, x0:x0 + n], ALU.mult, [KP_b[1], ROPE_b], [TMP_b[1]])
            tt(KPRO[0:64, x0:x0 + n], TMP[0:64, 0, 0:n], TMP[0:64, 1, 0:n], ALU.add, TMP_b, [KPRO_b])
        else:
            cp(KPRO[0:64, x0:x0 + n], KP[0:64, 0, 0:n], [KP_b[0]], [KPRO_b])
    dump("cqn", CQN[:, :, :], CQN_b, [128, 6, NL], BF16)
    dump("ckvn", CKVN[:, :, :], CKVN_b, [128, 4, NCEN], BF16)
    dump("kpro", KPRO[0:64, :], [KPRO_b], [64, NCEN], BF16)
    load(kv_in[0:512, :].rearrange("(k p) c -> p k c", p=128), CKVN[:, :, :], [kv_in_b], CKVN_b)
    load(kv_in[512:576, :], KPRO[0:64, :], [kv_in_b], [KPRO_b])
    prefetch(NRING - 1)
    collective([list(range(8))], kv_in, kv_in_b, kv_out, kv_out_b)
    C.barrier(["pe", "act", "dve", "sp"])

    ph.top = p34_base
    KVL = ph.alloc([128, 4, NKEY], BF16, "kvl")
    KVL_b = Buf()
    KPR = ph.alloc([128, NKEY], BF16, "kpr")
    KPR_b = Buf()
    p4_mark = ph.mark()
    SA = ph.alloc([128, 2, 2, NCEN], BF16, "sa")
    SA_b = [[Buf(), Buf()], [Buf(), Buf()]]
    TS = ph.alloc([128, 2, NCEN], F32, "tsel")
    TS_b = [Buf(), Buf()]
    idx = 0
    for jr in range(4):
        for kt in range(5):
            pp = idx % 2
            idx += 1
            npp = 128 if kt < 4 else 64
            r0 = kt * 128
            load(SA[0:npp, pp, 0, :], kv_out[jr * 640 + r0:jr * 640 + r0 + npp, :], [SA_b[pp][0]], [kv_out_b])
            load(SA[0:npp, pp, 1, :], kv_out[(4 + jr) * 640 + r0:(4 + jr) * 640 + r0 + npp, :], [SA_b[pp][1]],
                 [kv_out_b])
            parts = [(0, NL, jr * NL)] + ([(NL, NCX, 4 * NL + jr * NCX)] if jr < 2 else [])
            for (s0, n, d0) in parts:
                ts(TS[0:npp, pp, s0:s0 + n], SA[0:npp, pp, 0, s0:s0 + n], BSEL[0:npp, 0:1], None, ALU.mult, None,
                   [SA_b[pp][0], BSEL_b], [TS_b[pp]])
                if kt < 4:
                    dest, destb = KVL[:, kt, d0:d0 + n], KVL_b
                else:
                    dest, destb = KPR[0:64, d0:d0 + n], KPR_b
                stt(dest, SA[0:npp, pp, 1, s0:s0 + n], BSEL[0:npp, 1:2], TS[0:npp, pp, s0:s0 + n], ALU.mult, ALU.add,
                    [SA_b[pp][1], BSEL_b, TS_b[pp]], [destb])
    C.barrier(["pe", "act", "dve", "sp"])
    ph.top = p4_mark
    KH = ph.alloc([128, NKEY], BF16, "kh")
    KH_b = [Buf() for _ in range(9)]
    VH = ph.alloc([128, NKT, 128], BF16, "vh")
    VH_b = [Buf() for _ in range(9)]
    QN = ph.alloc([128, NL], BF16, "qn")
    QN_b = Buf()
    QP = ph.alloc([128, NL], BF16, "qp")
    QP_b = Buf()
    ET = ph.alloc([128, 3, 512], BF16, "et")
    ET_b = [Buf(), Buf(), Buf()]
    OH = ph.alloc([128, NL], BF16, "oh")
    OH_b = Buf()
    TQ = ph.alloc([128, 2, 512], F32, "tq")
    TQ_b = [Buf(), Buf()]
    RC = TQ[:, 0, :]
    RC_b = TQ_b[0]
    dump("kvl", KVL[:, :, :], [KVL_b], [128, 4, NKEY], BF16)
    dump("kpr", KPR[0:64, :], [KPR_b], [64, NKEY], BF16)
    if stop_after == "p3":
        finish()
        return nc, C
    PS_POOL[0] = 6
    ps_rr[0] = 0
    KB = col_blocks(NKEY)
    for h in range(16):
        WA, WAb = next_w("hA")
        WQ = WA[:, 0:1536].rearrange("p (k c) -> p k c", k=6)
        WKV = WA[:, 1536:2560].rearrange("p (k c) -> p k c", k=4)
        for bi, (c0, n) in enumerate(KB):
            pst, psb = next_ps()
            for kt in range(4):
                mm(pst[:, 0:n], WKV[:, kt, 0:128], KVL[:, kt, c0:c0 + n], kt == 0, kt == 3, [WAb, KVL_b], [psb])
            act(KH[:, c0:c0 + n], pst[:, 0:n], AF.Copy, [psb], [KH_b[bi]])
        for t0 in range(0, NKT, 4):
            tl = list(range(t0, min(t0 + 4, NKT)))
            pst, psb = next_ps()
            for jj, t in enumerate(tl):
                for kt in range(4):
                    mm(pst[:, jj * 128:(jj + 1) * 128], KVL[:, kt, t * 128:(t + 1) * 128], WKV[:, kt, 128:256],
                       kt == 0, kt == 3, [WAb, KVL_b], [psb])
            cp(VH[:, t0:t0 + len(tl), :], pst[:, 0:128 * len(tl)].rearrange("p (a b) -> p a b", b=128), [psb],
               [VH_b[t0 // 4]])
        for qb in range(2):
            q0 = qb * 512
            pst, psb = next_ps()
            for kt in range(6):
                mm(pst[:, :], WQ[:, kt, 0:128], CQN[:, kt, q0:q0 + 512], kt == 0, kt == 5, [WAb, CQN_b[kt]], [psb])
            act(QN[:, q0:q0 + 512], pst[:, :], AF.Copy, [psb], [QN_b])
            for half in range(2):
                pst, psb = next_ps()
                for kt in range(6):
                    mm(pst[0:64, :], WQ[:, kt, 128 + half * 64:192 + half * 64], CQN[:, kt, q0:q0 + 512], kt == 0,
                       kt == 5, [WAb, CQN_b[kt]], [psb])
                tt(TQ[0:64, half, :], pst[0:64, :], ROPE[0:64, half, q0:q0 + 512], ALU.mult, [psb, ROPE_b], [TQ_b[half]])
            tt(QP[0:64, q0:q0 + 512], TQ[0:64, 0, :], TQ[0:64, 1, :], ALU.add, TQ_b, [QP_b])
        if h == 0:
            dump("kh0", KH[:, :], KH_b, [128, NKEY], BF16)
            dump("vh0", VH[:, :, :], VH_b, [128, NKT, 128], BF16)
            dump("qn0", QN[:, :], [QN_b], [128, NL], BF16)
            dump("qp0", QP[0:64, :], [QP_b], [64, NL], BF16)
        for qb in range(2):
            q0 = qb * 512
            o_ps, o_b = ps_t[6], ps_b[6]
            d_ps, d_b = ps_t[7], ps_b[7]
            def scores(t):
                pst, psb = next_ps()
                mm(pst[:, :], KH[:, t * 128:(t + 1) * 128], QN[:, q0:q0 + 512], True, False, KH_b + [QN_b], [psb])
                mm(pst[:, :], KPR[0:64, t * 128:(t + 1) * 128], QP[0:64, q0:q0 + 512], False, True, [KPR_b, QP_b], [psb])
                return pst, psb
            cur = [scores(0), scores(1)]
            for t in range(NKT):
                pst, psb = cur.pop(0)
                if t + 2 < NKT:
                    cur.append(scores(t + 2))
                j = t % 3
                act(ET[:, j, :], pst[:, :], AF.Exp, [psb], [ET_b[j]], scale=ATT_SCALE)
                mm(o_ps[:, :], VH[:, t, :], ET[:, j, :], t == 0, t == NKT - 1, [VH_b[t // 4], ET_b[j]], [o_b])
                mm(d_ps[:, :], ones[:, :], ET[:, j, :], t == 0, t == NKT - 1, [ones_b, ET_b[j]], [d_b])
            recip(RC, d_ps[:, :], [d_b], [RC_b])
            tt(OH[:, q0:q0 + 512], o_ps[:, :], RC, ALU.mult, [o_b, RC_b], [OH_b])
        if h == 0:
            dump("oh0", OH[:, :], [OH_b], [128, NL], BF16)
        WB, WBb = next_w("hB")
        for mt in range(KT):
            for qb in range(2):
                q0 = qb * 512
                pst, psb = next_ps()
                mm(pst[:, :], WB[:, mt * 128:(mt + 1) * 128], OH[:, q0:q0 + 512], True, True, [WBb, OH_b], [psb])
                stt(XL[:, mt, q0:q0 + 512], pst[:, :], modv(1, 0, 2, mt), XL[:, mt, q0:q0 + 512], ALU.mult, ALU.add,
                    [psb, MOD_b, XL_b[mt][qb]], [XL_b[mt][qb]])
    PS_POOL[0] = 8
    dump("xl3", XL[:, :, 0:NL], XL_all, [128, KT, NL])
    if stop_after == "p4":
        finish()
        return nc, C
    C.barrier(["pe", "act", "dve", "sp"])

    ffn(1, False, 1)
    C.barrier(["pe", "act", "dve", "sp"])
    ph.top = ph_base
    SQ = ph.alloc([128, 2, 512], BF16, "sq")
    SQ_b = [Buf(), Buf()]
    RSTD = ph.alloc([128, 512], F32, "rstd")
    RSTD_b = Buf()
    OS = ph.alloc([128, 2, 512], F32, "os")
    OS_b = [Buf(), Buf()]
    for qb in range(2):
        q0 = qb * 512
        rms_rstd([XL[:, kt, q0:q0 + 512] for kt in range(KT)], [XL_b[kt][qb] for kt in range(KT)], 512, "m2048", SQ, SQ_b,
                 RSTD[:, :], RSTD_b)
        for kt in range(KT):
            j = kt % 2
            stt(OS[:, j, :], XL[:, kt, q0:q0 + 512], NG[:, 4, kt:kt + 1], RSTD[:, :], ALU.mult, ALU.mult,
                [XL_b[kt][qb], NG_b, RSTD_b], [OS_b[j]])
            load(outT[kt * 128:(kt + 1) * 128, q0:q0 + 512], OS[:, j, :], [Buf()], [OS_b[j]])
    finish()
    C.plan_rec = plan_rec
    return nc, C


def emit_program(nc, C):
    with nc.Block() as block:
        @block.tensor
        def _(e):
            C.replay(C.queues["pe"], e)

        @block.scalar
        def _(e):
            C.replay(C.queues["act"], e)

        @block.vector
        def _(e):
            C.replay(C.queues["dve"], e)

        @block.gpsimd
        def _(e):
            C.replay(C.queues["pool"], e)

        @block.sync
        def _(e):
            C.replay(C.queues["sp"], e)
    return nc


def _slots_kn(W, cols):
    K, N = W.shape
    kt, ns = K // 128, N // cols
    return np.ascontiguousarray(W.reshape(kt, 128, ns, cols).transpose(2, 1, 0, 3)).reshape(ns, 128, kt * cols)


def _fm(v):
    return np.ascontiguousarray(v.reshape(-1, 128).T)


def _bc(v):
    return np.ascontiguousarray(np.broadcast_to(v[None, :], (128, v.shape[0])))


def build_wstream(inp, order):
    ws = np.zeros((len(order), 128, SLOT), np.float32)
    src = {}
    src["ada0"] = list(_slots_kn(inp["w_ada"][0], 256))
    src["ada1"] = list(_slots_kn(inp["w_ada"][1], 256))
    w_in = inp["ab_w_in"][0]
    o = list(range(0, 8))
    for pr in range(4):
        o += [8 + pr, 12 + pr]
    win_slots = _slots_kn(w_in, 256)
    src["win"] = [win_slots[i] for i in o]
    src["wout"] = list(_slots_kn(inp["ab_w_out"][0], 256))
    for l in range(2):
        wup = inp["ffn_w_up"][l]
        src[f"f{l}g"] = list(_slots_kn(wup[:, :DFF], 256))
        src[f"f{l}u"] = list(_slots_kn(wup[:, DFF:], 256))
        src[f"f{l}d"] = list(inp["ffn_w_down"][l].reshape(NFCH, 2, 128, D).transpose(0, 2, 1, 3).reshape(NFCH, 128, 2 * D))
    mw = inp["mla_w_in"][0]
    kpe = mw[:, 1280:1344]
    mw2 = np.concatenate([mw, kpe[:, 32:64], kpe[:, 0:32], np.zeros((D, 128), np.float32)], axis=1)
    mslots = _slots_kn(mw2, 256)
    src["mwin"] = []
    for (_, _, s0, s1) in MW_BLOCKS:
        src["mwin"] += [mslots[s] for s in range(s0, s1)]
    wuq = inp["mla_w_uq"][0]
    wukv = inp["mla_w_ukv"][0]
    wo = inp["mla_w_o"][0]
    src["hA"], src["hB"] = [], []
    for h in range(16):
        qh = wuq[:, h * 192:(h + 1) * 192]
        uq = np.concatenate([qh[:, 0:128], qh[:, 128:192], qh[:, 160:192], qh[:, 128:160]], axis=1)
        ukv = wukv[:, h * 256:(h + 1) * 256]
        a = np.concatenate([uq.reshape(6, 128, 256).transpose(1, 0, 2).reshape(128, 1536),
                            ukv.reshape(4, 128, 256).transpose(1, 0, 2).reshape(128, 1024)], axis=1)
        src["hA"].append(a)
        src["hB"].append(np.ascontiguousarray(wo[h * 128:(h + 1) * 128, :]))
    pos = {k: 0 for k in src}
    for i, tag in enumerate(order):
        arr = src[tag][pos[tag]]
        pos[tag] += 1
        ws[i, :, :arr.shape[1]] = arr
    for k in src:
        assert pos[k] == len(src[k]), (k, pos[k], len(src[k]))
    return ws


def rope_tables():
    pos = np.arange(L)
    row = (pos // 64).astype(np.float32)
    col = (pos % 64).astype(np.float32)
    n_freq = 16
    inv = (np.float32(10000.0) ** (-np.arange(n_freq, dtype=np.float32) / n_freq)).astype(np.float32)
    ang = np.concatenate([row[:, None] * inv, col[:, None] * inv], axis=-1).astype(np.float32)
    cos, sin = np.cos(ang).astype(np.float32), np.sin(ang).astype(np.float32)
    cos2 = np.concatenate([cos, cos], axis=1).T
    sins = np.concatenate([-sin, sin], axis=1).T
    return cos2, sins


def build_in_maps(inp, order):
    inp = {k: np.asarray(v, dtype=np.float32) for k, v in inp.items()}
    ws = build_wstream(inp, order)
    cos2, sins = rope_tables()
    shared = {
        "wstream": ws,
        "bada": np.ascontiguousarray(np.stack([_fm(inp["b_ada"][0]), _fm(inp["b_ada"][1])], axis=1)),
        "ng": np.ascontiguousarray(np.stack([_fm(inp["norm1_g"][0]), _fm(inp["norm1_g"][1]), _fm(inp["norm2_g"][0]),
                                             _fm(inp["norm2_g"][1]), _fm(inp["final_norm_g"])], axis=1)),
        "bin_fm": _fm(inp["ab_b_in"][0]),
        "binv": _bc(inp["ab_b_in"][0][1024:2048]),
        "alng": _bc(inp["a_ln_g"][0]),
        "alnb": _bc(inp["a_ln_b"][0]),
        "wst": np.ascontiguousarray(inp["a_w_s"][0].transpose(2, 0, 1)),
        "bsb": np.ascontiguousarray(np.broadcast_to(inp["a_b_s"][0][None], (128, 8, 128))),
        "bcw": np.ascontiguousarray(inp["b_conv_w"][0].reshape(31, 8, 128).transpose(2, 1, 0)),
        "bsm": np.ascontiguousarray(np.stack([_fm(inp["b_conv_b"][0]), _fm(inp["b_ln_g"][0]), _fm(inp["b_ln_b"][0])],
                                             axis=2)),
        "fcw": np.ascontiguousarray(np.stack([
            np.concatenate([inp["ffn_conv_w"][l].reshape(3, 44, 128).transpose(2, 1, 0),
                            _fm(inp["ffn_conv_b"][l])[:, :, None]], axis=2) for l in range(2)], axis=1)),
        "qkg": np.ascontiguousarray(np.concatenate([_fm(inp["mla_q_norm_g"][0]), _fm(inp["mla_kv_norm_g"][0])], axis=1)),
    }
    x, ctx, c, c_ctx = inp["x"], inp["ctx"], inp["c"], inp["c_ctx"]
    maps = []
    for k in range(8):
        b, q = k // 4, k % 4
        cc = q % 2
        xe = np.zeros((NM, D), np.float32)
        t0, t1 = q * NL - HB, (q + 1) * NL + HB
        s0, s1 = max(t0, 0), min(t1, L)
        xe[s0 - t0:s1 - t0] = x[b, s0:s1]
        base = NL + 2 * HB
        t0, t1 = cc * NCX - HB, (cc + 1) * NCX + HB
        s0, s1 = max(t0, 0), min(t1, 256)
        xe[base + s0 - t0:base + s1 - t0] = ctx[b, s0:s1]
        maskb = np.zeros((128, 4, HB), np.float32)
        maskb[:, 0] = 1.0 if q > 0 else 0.0
        maskb[:, 1] = 1.0 if q < 3 else 0.0
        maskb[:, 2] = 1.0 if cc == 1 else 0.0
        maskb[:, 3] = 1.0 if cc == 0 else 0.0
        oh = np.zeros((128, 4, 8), np.float32)
        if q > 0:
            oh[:, 0, k - 1] = 1.0
        if q < 3:
            oh[:, 1, k + 1] = 1.0
        if cc == 1:
            oh[:, 2, 4 * b] = 1.0
        if cc == 0:
            oh[:, 3, 4 * b + 1] = 1.0
        m = dict(shared)
        m["xT"] = np.ascontiguousarray(xe.T)
        m["cmat"] = np.ascontiguousarray(np.stack([_fm(c[b]), _fm(c_ctx)], axis=2))
        m["maskb"] = maskb
        m["ohs"] = oh
        bs = np.zeros((128, 2), np.float32)
        bs[:, b] = 1.0
        m["bsel"] = bs
        m["ropet"] = np.ascontiguousarray(np.stack([cos2[:, q * NL:(q + 1) * NL], sins[:, q * NL:(q + 1) * NL]], axis=1))
        maps.append(m)
    return maps


_NC_CACHE = {}


def kernel(**inputs):
    if "nc" not in _NC_CACHE:
        _, C0 = build_nc()
        nc, C = build_nc(order=C0.plan_rec)
        emit_program(nc, C)
        _NC_CACHE["nc"] = nc
        _NC_CACHE["order"] = list(C0.plan_rec)
    nc = _NC_CACHE["nc"]
    maps = build_in_maps(inputs, _NC_CACHE["order"])
    res = run_bass_kernel_spmd(nc, maps, core_ids=list(range(8)))
    out = np.zeros((2, L, D), np.float32)
    for k in range(8):
        b, q = k // 4, k % 4
        out[b, q * NL:(q + 1) * NL, :] = res.results[k]["outT"].T
    return out
```

```python
import numpy as np
import concourse.bass as bass
import concourse.mybir as mybir
from concourse.bass_utils import run_bass_kernel_spmd

F32 = mybir.dt.float32
BF16 = mybir.dt.bfloat16
AF = mybir.ActivationFunctionType
ALU = mybir.AluOpType

D = 2048
KT = 16
L = 4096
NL = 1024
NCX = 128
NCEN = NL + NCX
HB = 15
NM = (NL + 2 * HB) + (NCX + 2 * HB)
LOFF = HB
COFF = NL + 2 * HB + HB
DFF = 5632
FCH = 256
NFCH = DFF // FCH
SLOT = 4096
NRING = 4
EPS = 1e-6
NKEY = 4096 + 256
NKT = NKEY // 128
ATT_SCALE = 192.0 ** -0.5
GELU_C = 1.5957691216057308


class Buf:
    __slots__ = ("w", "r", "name")

    def __init__(self, name=""):
        self.w = None
        self.r = []
        self.name = name


class Q:
    def __init__(self, name, key, sem, is_pe=False):
        self.name, self.key, self.sem, self.is_pe = name, key, sem, is_pe
        self.cnt = 0
        self.ops = []
        self.seen = {}


class Ctx:
    def __init__(self, nc):
        self.nc = nc
        self.sems = []
        self.queues = {}
        self.dma_sems = []
        self.dma_rr = 0
        self.n_inst = 0

    def new_sem(self, name):
        s = self.nc.alloc_semaphore(name)
        self.sems.append(s)
        return len(self.sems) - 1

    def add_queue(self, name, is_pe=False):
        key = self.new_sem("q_" + name)
        q = Q(name, key, self.sems[key], is_pe)
        self.queues[name] = q
        return q

    def add_dma_sems(self, n):
        for i in range(n):
            self.dma_sems.append([self.new_sem(f"dma{i}"), 0])

    def _waits(self, q, deps, skip_same):
        for sk, v in deps.items():
            if skip_same and sk == q.key:
                continue
            if q.seen.get(sk, 0) >= v:
                continue
            q.seen[sk] = v
            q.ops.append(("wait", sk, v))

    @staticmethod
    def _collect(reads, writes):
        deps = {}

        def add(ev):
            if ev is not None and deps.get(ev[0], 0) < ev[1]:
                deps[ev[0]] = ev[1]
        for b in reads:
            add(b.w)
        for b in writes:
            add(b.w)
            for r in b.r:
                add(r)
        return deps

    def emit(self, q, fn, reads=(), writes=()):
        deps = self._collect(reads, writes)
        self._waits(q, deps, skip_same=q.is_pe)
        q.cnt += 1
        ev = (q.key, q.cnt)
        q.ops.append(("op", fn))
        self.n_inst += 1
        for b in writes:
            b.w = ev
            b.r = []
        for b in reads:
            if b not in writes:
                b.r.append(ev)

    def dma(self, q, out_ap, in_ap, reads=(), writes=()):
        deps = self._collect(reads, writes)
        ds = self.dma_sems[self.dma_rr]
        self.dma_rr = (self.dma_rr + 1) % len(self.dma_sems)
        if ds[1] > 0 and deps.get(ds[0], 0) < ds[1]:
            deps[ds[0]] = ds[1]
        self._waits(q, deps, skip_same=False)
        ds[1] += 16
        ev = (ds[0], ds[1])
        sem = self.sems[ds[0]]
        q.ops.append(("dma", out_ap, in_ap, sem))
        self.n_inst += 1
        for b in writes:
            b.w = ev
            b.r = []
        for b in reads:
            if b not in writes:
                b.r.append(ev)
        return ev

    def barrier(self, names):
        tot = {}
        for q in self.queues.values():
            if q.cnt:
                tot[q.key] = q.cnt
        for k, c in self.dma_sems:
            if c:
                tot[k] = c
        for n in names:
            q = self.queues[n]
            self._waits(q, dict(tot), skip_same=True)

    def replay(self, q, eng):
        sems = self.sems
        for op in q.ops:
            if op[0] == "wait":
                eng.wait_ge(sems[op[1]], op[2])
            elif op[0] == "op":
                op[1](eng).then_inc(q.sem, 1)
            elif op[0] == "cc":
                op[1](eng).then_inc(self.cc_sem)
            else:
                eng.dma_start(out=op[1], in_=op[2]).then_inc(op[3], 16)


class Arena:
    def __init__(self, nc, base, limit):
        self.nc, self.base, self.limit = nc, base, limit
        self.top = base
        self.n = 0

    def alloc(self, shape, dtype, name=None):
        esz = 2 if dtype == BF16 else 4
        per = esz
        for s in shape[1:]:
            per *= s
        off = (self.top + 31) // 32 * 32
        assert off + per <= self.limit, f"arena overflow {name}: {off}+{per} > {self.limit}"
        self.top = off + per
        self.n += 1
        return self.nc.alloc_sbuf_tensor_at(f"{name or 't'}_{self.n}_{off}", list(shape), dtype, offset=off)

    def mark(self):
        return self.top

    def reset(self, m):
        self.top = m


MW_BLOCKS = [(0, 512, 0, 6), (512, 512, 0, 6), (1024, 128, 3, 6)]


def slot_plan():
    plan = []
    plan += [("ada0", SLOT)] * 48
    plan += [("win", SLOT)] * 16
    plan += [("wout", SLOT)] * 8
    plan += [("ada1", SLOT)] * 48
    plan += [("f0" + k, SLOT) for (k, _) in ffn_order()]
    for (_, _, s0, s1) in MW_BLOCKS:
        plan += [("mwin", SLOT)] * (s1 - s0)
    for h in range(16):
        plan += [("hA", 2560), ("hB", 2048)]
    plan += [("f1" + k, SLOT) for (k, _) in ffn_order()]
    return plan


def ffn_order():
    o = [("g", 0), ("u", 0)]
    for f in range(1, NFCH):
        o += [("g", f), ("u", f), ("d", f - 1)]
    o += [("d", NFCH - 1)]
    return o


def col_blocks(n, nb=None):
    if nb is None:
        nb = (n + 511) // 512
    base = (n + nb - 1) // nb
    out = []
    c = 0
    while c < n:
        w = min(base, n - c)
        out.append((c, w))
        c += w
    return out


TAG_N = {"hA": 2560, "hB": 2048}


def build_nc(debug=(), stop_after=None, order=None):
    nc = bass.Bass("TRN2", target_bir_lowering=False)
    plan = slot_plan()
    NS = len(plan)
    record = order is None
    plan_rec = [] if record else list(order)
    if stop_after in ("p1", "p2"):
        NS = 48 + 16 + 8 + 48 + 3 * NFCH
    if stop_after == "p3":
        NS = 48 + 16 + 8 + 48 + 3 * NFCH + 15 + 3

    def din(name, shape, dt=F32):
        return nc.dram_tensor(name, list(shape), dt, kind="ExternalInput")

    wstream = din("wstream", [NS, 128, SLOT])
    xT = din("xT", [D, NM])
    cmat = din("cmat", [128, KT, 2])
    bada = din("bada", [128, 2, 96])
    ng = din("ng", [128, 5, KT])
    bin_fm = din("bin_fm", [128, 32])
    binv = din("binv", [128, 1024])
    alng = din("alng", [128, 1024])
    alnb = din("alnb", [128, 1024])
    wst = din("wst", [128, 8, 128])
    bsb = din("bsb", [128, 8, 128])
    bcw = din("bcw", [128, 8, 31])
    bsm = din("bsm", [128, 8, 3])
    maskb = din("maskb", [128, 4, HB])
    fcw = din("fcw", [128, 2, 44, 4])
    qkg = din("qkg", [128, 10])
    ropet = din("ropet", [64, 2, NL])
    ohs = din("ohs", [128, 4, 8])
    bsel = din("bsel", [128, 2])
    outT = nc.dram_tensor("outT", [D, NL], F32, kind="ExternalOutput")

    cc_in = [nc.dram_tensor(f"cc_in{i}", [128, 64], BF16) for i in range(2)]
    cc_out = [nc.dram_tensor(f"cc_out{i}", [8 * 128, 64], BF16) for i in range(2)]
    cc_in_b = [Buf(), Buf()]
    cc_out_b = [Buf(), Buf()]
    kv_in = nc.dram_tensor("kv_in", [640, NCEN], BF16)
    kv_out = nc.dram_tensor("kv_out", [8 * 640, NCEN], BF16)
    kv_in_b, kv_out_b = Buf(), Buf()

    C = Ctx(nc)
    PE = C.add_queue("pe", is_pe=True)
    ACT = C.add_queue("act")
    DVE = C.add_queue("dve")
    POOL = C.add_queue("pool")
    SP = C.add_queue("sp")
    C.add_dma_sems(24)
    cc_sem = nc.alloc_semaphore("cc_sem")
    cc_key = len(C.sems)
    C.sems.append(cc_sem)
    C.cc_sem = cc_sem
    cc_count = [0]

    SB_LIMIT = int(nc.SBUF_PARTITION_SIZE_BYTES)
    SB_BASE = (SB_LIMIT - int(nc.sbuf_bytes_remaining) + 63) // 64 * 64
    ar = Arena(nc, SB_BASE, SB_LIMIT)

    ring_t = [ar.alloc([128, SLOT], BF16, "ring") for _ in range(NRING)]
    ring_b = [Buf(f"ring{i}") for i in range(NRING)]
    ones = ar.alloc([128, 128], BF16, "ones")
    ones_b = Buf("ones")
    onesm = {}
    for nm in ("m2048", "m1024", "m768", "m512"):
        onesm[nm] = (ar.alloc([128, 128], BF16, nm), Buf(nm))
    MOD = ar.alloc([128, 2, 2, 96], F32, "mod")
    MOD_b = Buf("mod")
    AB = ar.alloc([128, 2, 2, 2, KT], F32, "ab")
    AB_b = Buf("ab")
    NG = ar.alloc([128, 5, KT], F32, "ng")
    NG_b = Buf("ng")
    FCW = ar.alloc([128, 2, 44, 4], F32, "fcw")
    FCW_b = Buf("fcw")
    OHS = ar.alloc([128, 4, 8], F32, "ohs")
    OHS_b = Buf("ohs")
    QKG = ar.alloc([128, 10], F32, "qkg")
    QKG_b = Buf("qkg")
    BSEL = ar.alloc([128, 2], F32, "bsel")
    BSEL_b = Buf("bsel")
    CM = ar.alloc([128, KT, 2], F32, "cm")
    CM_b = Buf("cm")
    BADA = ar.alloc([128, 2, 96], F32, "bada")
    BADA_b = Buf("bada")
    SC = ar.alloc([128, KT, 2], BF16, "sc")
    SC_b = Buf("sc")
    EPSB = ar.alloc([128, 1], F32, "epsb")
    EPSB_b = Buf("epsb")
    xl_off = (ar.top + 31) // 32 * 32
    XL = ar.alloc([128, KT, NCEN], F32, "xl")
    XL_b = [[Buf(f"xl{i}_{j}") for j in range(3)] for i in range(KT)]
    XL_all = [b for row in XL_b for b in row]
    xl_end = ar.top
    ph_base = ar.top

    ps_t = [nc.alloc_psum_tensor(f"ps{i}", [128, 512], F32) for i in range(8)]
    ps_b = [Buf(f"ps{i}") for i in range(8)]
    ps_rr = [0]
    PS_POOL = [8]

    def next_ps():
        i = ps_rr[0] % PS_POOL[0]
        ps_rr[0] = (i + 1) % PS_POOL[0]
        return ps_t[i], ps_b[i]

    def mm(out, lhsT, rhs, start, stop, reads, writes):
        C.emit(PE, lambda e: e.matmul(out, lhsT=lhsT, rhs=rhs, start=start, stop=stop), reads, writes)

    def act(out, in_, func, reads, writes, bias=0.0, scale=1.0):
        C.emit(ACT, lambda e: e.activation(out=out, in_=in_, func=func, bias=bias, scale=scale), reads, writes)

    def tt(out, in0, in1, op, reads, writes, q=None):
        C.emit(q or DVE, lambda e: e.tensor_tensor(out=out, in0=in0, in1=in1, op=op), reads, writes)

    def ts(out, in0, s1, s2, op0, op1, reads, writes, q=None):
        if s2 is None:
            C.emit(q or DVE, lambda e: e.tensor_scalar(out=out, in0=in0, scalar1=s1, scalar2=None, op0=op0),
                   reads, writes)
        else:
            C.emit(q or DVE, lambda e: e.tensor_scalar(out=out, in0=in0, scalar1=s1, scalar2=s2, op0=op0, op1=op1),
                   reads, writes)

    def stt(out, in0, scalar, in1, op0, op1, reads, writes, q=None):
        C.emit(q or DVE, lambda e: e.scalar_tensor_tensor(out=out, in0=in0, scalar=scalar, in1=in1, op0=op0, op1=op1),
               reads, writes)

    def cp(out, in_, reads, writes, q=None):
        C.emit(q or DVE, lambda e: e.tensor_copy(out=out, in_=in_), reads, writes)

    def recip(out, in_, reads, writes):
        C.emit(DVE, lambda e: e.reciprocal(out=out, in_=in_), reads, writes)

    def rsum(out, in_, reads, writes):
        C.emit(DVE, lambda e: e.reduce_sum(out=out, in_=in_, axis=mybir.AxisListType.X), reads, writes)

    def memset(out, val, writes, q=None):
        C.emit(q or DVE, lambda e: e.memset(out, val), (), writes)

    def load(out, in_, writes, reads=(), q=None):
        C.dma(q or SP, out, in_, reads, writes)

    def collective(groups, in_t, in_b, out_t, out_b):
        deps = C._collect([in_b], [out_b])
        C._waits(POOL, deps, skip_same=False)
        cc_count[0] += 1
        ev = (cc_key, cc_count[0])
        POOL.ops.append(("cc", lambda e: e.collective_compute(
            "AllGather", ALU.bypass, replica_groups=groups, ins=[in_t.ap().opt()], outs=[out_t.ap().opt()])))
        out_b.w = ev
        out_b.r = []
        in_b.r.append(ev)

    wi = [0]
    wq = [0]

    def prefetch(k):
        if record and k > 1:
            return
        while wq[0] < min(wi[0] + k, len(plan_rec)):
            i = wq[0]
            n = TAG_N.get(plan_rec[i], SLOT)
            r = i % NRING
            C.dma(POOL, ring_t[r][:, 0:n], wstream[i, :, 0:n], (), [ring_b[r]])
            wq[0] += 1

    def next_w(tag):
        i = wi[0]
        if record:
            plan_rec.append(tag)
        assert plan_rec[i] == tag, (i, plan_rec[i], tag)
        prefetch(1)
        wi[0] += 1
        r = i % NRING
        return ring_t[r], ring_b[r]

    def dump(name, ap, bufs, shape, dt=F32):
        if name not in debug:
            return
        o = nc.dram_tensor("dbg_" + name, list(shape), dt, kind="ExternalOutput")
        C.dma(SP, o.ap(), ap, bufs, [Buf()])

    def finish():
        C.barrier(["sp"])

    memset(ones[:, :], 1.0, [ones_b])
    for nm, val in (("m2048", 1.0 / 2048), ("m1024", 1.0 / 1024), ("m768", 1.0 / 768), ("m512", 1.0 / 512)):
        memset(onesm[nm][0][:, :], val, [onesm[nm][1]])
    memset(EPSB[:, :], EPS, [EPSB_b])
    load(NG[:, :, :], ng.ap(), [NG_b])
    load(FCW[:, :, :, :], fcw.ap(), [FCW_b])
    load(OHS[:, :, :], ohs.ap(), [OHS_b])
    load(QKG[:, :], qkg.ap(), [QKG_b])
    load(BSEL[:, :], bsel.ap(), [BSEL_b])
    load(CM[:, :, :], cmat.ap(), [CM_b])
    load(BADA[:, :, :], bada.ap(), [BADA_b])
    act(SC[:, :, :], CM[:, :, :], AF.Silu, [CM_b], [SC_b])

    ada_s = [0, 0]

    def ada_steps(l, k):
        pst, psb = ps_t[7], ps_b[7]
        for _ in range(k):
            sidx = ada_s[l]
            if sidx >= 48:
                return
            ada_s[l] += 1
            W, Wb = next_w(f"ada{l}")
            Wv = W[:, :].rearrange("p (k c) -> p k c", k=KT)
            for m in range(2):
                mt = sidx * 2 + m
                for kt in range(KT):
                    mm(pst[:, mt * 2:mt * 2 + 2], Wv[:, kt, m * 128:(m + 1) * 128], SC[:, kt, :],
                       kt == 0, kt == KT - 1, [Wb, SC_b], [psb])

    def ada_finalize(l, j0, j1):
        pst, psb = ps_t[7], ps_b[7]
        psv = pst[:, 0:192].rearrange("p (m r) -> p m r", r=2)
        for r in range(2):
            tt(MOD[:, l, r, j0 * 16:j1 * 16], psv[:, j0 * 16:j1 * 16, r], BADA[:, l, j0 * 16:j1 * 16], ALU.add,
               [psb, BADA_b], [MOD_b])

    def ada_ab(l, which):
        for r in range(2):
            if which == 0:
                stt(AB[:, l, r, 0, :], MOD[:, l, r, 16:32], 1.0, NG[:, l, :], ALU.add, ALU.mult, [MOD_b, NG_b], [AB_b])
            else:
                stt(AB[:, l, r, 1, :], MOD[:, l, r, 64:80], 1.0, NG[:, 2 + l, :], ALU.add, ALU.mult, [MOD_b, NG_b],
                    [AB_b])

    def modv(l, r, j, kt):
        return MOD[:, l, r, j * 16 + kt:j * 16 + kt + 1]

    def rms_rstd(x_aps, x_bufs, n, onesname, SQ, SQ_b, rstd_ap, rstd_b):
        pst, psb = next_ps()
        nk = len(x_aps)
        om, omb = onesm[onesname]
        for k in range(nk):
            j = k % 2
            act(SQ[:, j, 0:n], x_aps[k], AF.Square, [x_bufs[k]], [SQ_b[j]])
            mm(pst[:, 0:n], om[:, :], SQ[:, j, 0:n], k == 0, k == nk - 1, [omb, SQ_b[j]], [psb])
        act(rstd_ap, pst[:, 0:n], AF.Sqrt, [psb, EPSB_b], [rstd_b], bias=EPSB[:, 0:1])
        recip(rstd_ap, rstd_ap, [rstd_b], [rstd_b])

    def gelu_from(X, Xb, T, Tb, out, outb):
        act(T, X, AF.Square, [Xb], [Tb])
        ts(T, T, 0.044715, 1.0, ALU.mult, ALU.add, [Tb], [Tb])
        tt(T, T, X, ALU.mult, [Tb, Xb], [Tb])
        act(T, T, AF.Sigmoid, [Tb], [Tb], scale=GELU_C)
        tt(out, T, X, ALU.mult, [Tb, Xb], [outb])

    PS_POOL[0] = 7
    ada_steps(0, 16)
    ada_finalize(0, 0, 2)
    ada_ab(0, 0)

    ph = Arena(nc, ph_base, SB_LIMIT)
    ph.top = ph_base
    YA = ph.alloc([128, 8, NCEN], BF16, "ya")
    YA_b = [Buf(f"ya{i}") for i in range(8)]
    YB = ph.alloc([128, 8, NCEN], BF16, "yb")
    YB_b = [Buf(f"yb{i}") for i in range(8)]
    BINF = ph.alloc([128, 32], F32, "binf")
    BINF_b = Buf()
    BCW = ph.alloc([128, 8, 31], F32, "bcw")
    BCW_b = Buf()
    BSM = ph.alloc([128, 8, 3], F32, "bsm")
    BSM_b = Buf()
    MASKB = ph.alloc([128, 4, HB], F32, "maskb")
    MASKB_b = Buf()
    WST = ph.alloc([128, 8, 128], BF16, "wst")
    WST_b = Buf()
    BINV = ph.alloc([128, 1024], F32, "binv")
    BINV_b = Buf()
    ALNG = ph.alloc([128, 1024], F32, "alng")
    ALNG_b = Buf()
    ALNB = ph.alloc([128, 1024], F32, "alnb")
    ALNB_b = Buf()
    BSB = ph.alloc([128, 8, 128], F32, "bsb")
    BSB_b = Buf()
    SQ = ph.alloc([128, 2, 512], BF16, "sq")
    SQ_b = [Buf(), Buf()]
    RSTD = ph.alloc([128, 512], F32, "rstd")
    RSTD_b = Buf()
    TMP = ph.alloc([128, 2, 512], F32, "tmp")
    TMP_b = [Buf(), Buf()]
    TMP2 = ph.alloc([128, 2, 512], F32, "tmp2")
    TMP2_b = [Buf(), Buf()]
    XB = ph.alloc([128, 2, KT, 64], F32, "xb")
    XB_b = [Buf(), Buf()]
    STAT = ph.alloc([128, 16], F32, "stat")
    STAT_b = Buf()
    ACC = ph.alloc([128, 2, NCEN], F32, "acc")
    ph_acc = [ACC[:, 0, :], ACC[:, 1, :]]
    ph_acc_b = [[Buf(), Buf()], [Buf(), Buf()]]
    load(BINF[:, :], bin_fm.ap(), [BINF_b])
    load(BCW[:, :, :], bcw.ap(), [BCW_b])
    load(BSM[:, :, :], bsm.ap(), [BSM_b])
    load(MASKB[:, :, :], maskb.ap(), [MASKB_b])
    C.dma(POOL, WST[:, :, :], wst.ap(), (), [WST_b])
    load(BINV[:, :], binv.ap(), [BINV_b])
    load(ALNG[:, :], alng.ap(), [ALNG_b])
    load(ALNB[:, :], alnb.ap(), [ALNB_b])
    load(BSB[:, :, :], bsb.ap(), [BSB_b])
    xa = Arena(nc, xl_off, xl_end)
    HN = xa.alloc([128, KT, NM], BF16, "hn")
    HN_b = [Buf(f"hn{i}") for i in range(KT)]
    VN = xa.alloc([128, 9, 1024], BF16, "vn")
    VN_b = [Buf(f"vn{i}") for i in range(9)]
    HG = xa.alloc([128, 2, NM], BF16, "hg")
    HG_b = [Buf(), Buf()]
    ZA = xa.alloc([128, 2, NM], BF16, "za")
    ZA_b = [Buf(), Buf()]

    xTv = xT.ap().rearrange("(k p) c -> p k c", p=128)
    blk = [(c, min(64, NM - c)) for c in range(0, NM, 64)]
    for bi, (c0, n) in enumerate(blk):
        j = bi % 2
        load(XB[:, j, :, 0:n], xTv[:, :, c0:c0 + n], [XB_b[j]])
        rms_rstd([XB[:, j, kt, 0:n] for kt in range(KT)], [XB_b[j]] * KT, n, "m2048", SQ, SQ_b,
                 RSTD[:, 0:n], RSTD_b)
        segs = []
        lat_end = NL + 2 * HB
        if c0 < lat_end:
            segs.append((0, min(n, lat_end - c0), 0))
        if c0 + n > lat_end:
            s0 = max(0, lat_end - c0)
            segs.append((s0, n, 1))
        for kt in range(KT):
            for (a, b, r) in segs:
                tj = kt % 2
                stt(TMP[:, tj, a:b], XB[:, j, kt, a:b], AB[:, 0, r, 0, kt:kt + 1], RSTD[:, a:b], ALU.mult, ALU.mult,
                    [XB_b[j], AB_b, RSTD_b], [TMP_b[tj]])
                act(HN[:, kt, c0 + a:c0 + b], TMP[:, tj, a:b], AF.Identity, [TMP_b[tj], MOD_b], [HN_b[kt]],
                    bias=modv(0, r, 0, kt))
    dump("hn", HN[:, :, :], HN_b, [128, KT, NM], BF16)

    CEN = [(LOFF, 512, 0, 0), (LOFF + 512, 512, 512, 0), (COFF, 128, 1024, 1)]
    MBLK = col_blocks(NM, 3)

    for s in range(4):
        W, Wb = next_w("win")
        ada_steps(0, 2)
        Wv = W[:, :].rearrange("p (k c) -> p k c", k=KT)
        for m in range(2):
            mt = s * 2 + m
            for (mc0, n, cc0, r) in CEN:
                pst, psb = next_ps()
                for kt in range(KT):
                    mm(pst[:, 0:n], Wv[:, kt, m * 128:(m + 1) * 128], HN[:, kt, mc0:mc0 + n], kt == 0, kt == KT - 1,
                       [Wb, HN_b[kt]], [psb])
                j = (mt + (cc0 // 512)) % 2
                act(TMP[:, j, 0:n], pst[:, 0:n], AF.Identity, [psb, BINF_b], [TMP_b[j]], bias=BINF[:, mt:mt + 1])
                gelu_from(TMP[:, j, 0:n], TMP_b[j], TMP2[:, j, 0:n], TMP2_b[j], YA[:, mt, cc0:cc0 + n], YA_b[mt])
    dump("u", YA[:, :, :], YA_b, [128, 8, NCEN], BF16)

    chunk_cols = [LOFF + 128 * c for c in range(8)] + [COFF]
    for s in range(4):
        W, Wb = next_w("win")
        ada_steps(0, 2)
        Wv = W[:, :].rearrange("p (k c) -> p k c", k=KT)
        for c in range(9):
            pst, psb = next_ps()
            mc0 = chunk_cols[c]
            for kt in range(KT):
                mm(pst[:, 0:256], HN[:, kt, mc0:mc0 + 128], Wv[:, kt, :], kt == 0, kt == KT - 1,
                   [Wb, HN_b[kt]], [psb])
            j = c % 2
            tt(TMP[:, j, 0:256], pst[:, 0:256], BINV[:, s * 256:(s + 1) * 256], ALU.add, [psb, BINV_b], [TMP_b[j]])
            gelu_from(TMP[:, j, 0:256], TMP_b[j], TMP2[:, j, 0:256], TMP2_b[j],
                      VN[:, c, s * 256:(s + 1) * 256], VN_b[c])
    TMPf = TMP[:, :, :].rearrange("p a b -> p (a b)")
    TMP2f = TMP2[:, :, :].rearrange("p a b -> p (a b)")
    for c in range(9):
        rsum(STAT[:, 0:1], VN[:, c, :], [VN_b[c]], [STAT_b])
        act(TMPf, VN[:, c, :], AF.Square, [VN_b[c]], TMP_b)
        rsum(STAT[:, 1:2], TMPf, TMP_b, [STAT_b])
        ts(STAT[:, 2:3], STAT[:, 0:1], 1.0 / 1024, None, ALU.mult, None, [STAT_b], [STAT_b])
        tt(STAT[:, 3:4], STAT[:, 2:3], STAT[:, 2:3], ALU.mult, [STAT_b], [STAT_b])
        stt(STAT[:, 4:5], STAT[:, 1:2], 1.0 / 1024, STAT[:, 3:4], ALU.mult, ALU.subtract, [STAT_b], [STAT_b])
        act(STAT[:, 5:6], STAT[:, 4:5], AF.Sqrt, [STAT_b, EPSB_b], [STAT_b], bias=EPSB[:, 0:1])
        recip(STAT[:, 6:7], STAT[:, 5:6], [STAT_b], [STAT_b])
        ts(TMPf, VN[:, c, :], STAT[:, 2:3], STAT[:, 6:7], ALU.subtract, ALU.mult, [VN_b[c], STAT_b], TMP_b)
        tt(TMPf, TMPf, ALNG[:, :], ALU.mult, TMP_b + [ALNG_b], TMP_b)
        tt(VN[:, c, :], TMPf, ALNB[:, :], ALU.add, TMP_b + [ALNB_b], [VN_b[c]])
    dump("vn", VN[:, :, :], VN_b, [128, 9, 1024], BF16)

    for h in range(8):
        for g0 in (0, 4, 8):
            cs = list(range(g0, min(g0 + 4, 9)))
            pst, psb = next_ps()
            for jj, c in enumerate(cs):
                mm(pst[:, jj * 128:(jj + 1) * 128], VN[:, c, h * 128:(h + 1) * 128], WST[:, h, :], True, True,
                   [VN_b[c], WST_b], [psb])
            for jj, c in enumerate(cs):
                tj = c % 2
                tt(TMP[:, tj, 0:128], pst[:, jj * 128:(jj + 1) * 128], BSB[:, h, :], ALU.add, [psb, BSB_b], [TMP_b[tj]])
                tt(YA[:, h, c * 128:(c + 1) * 128], TMP[:, tj, 0:128], YA[:, h, c * 128:(c + 1) * 128], ALU.mult,
                   [TMP_b[tj], YA_b[h]], [YA_b[h]])
    dump("ya", YA[:, :, :], YA_b, [128, 8, NCEN], BF16)

    for pr in range(4):
        Wa, Wab = next_w("win")
        ada_steps(0, 2)
        Wav = Wa[:, :].rearrange("p (k c) -> p k c", k=KT)
        for m in range(2):
            ct = pr * 2 + m
            for (c0, n) in MBLK:
                pst, psb = next_ps()
                for kt in range(KT):
                    mm(pst[:, 0:n], Wav[:, kt, m * 128:(m + 1) * 128], HN[:, kt, c0:c0 + n], kt == 0, kt == KT - 1,
                       [Wab, HN_b[kt]], [psb])
                act(ZA[:, m, c0:c0 + n], pst[:, 0:n], AF.Identity, [psb, BINF_b], [ZA_b[m]],
                    bias=BINF[:, 16 + ct:17 + ct])
        Wg, Wgb = next_w("win")
        ada_steps(0, 2)
        Wgv = Wg[:, :].rearrange("p (k c) -> p k c", k=KT)
        for m in range(2):
            ct = pr * 2 + m
            for bi, (c0, n) in enumerate(MBLK):
                pst, psb = next_ps()
                for kt in range(KT):
                    mm(pst[:, 0:n], Wgv[:, kt, m * 128:(m + 1) * 128], HN[:, kt, c0:c0 + n], kt == 0, kt == KT - 1,
                       [Wgb, HN_b[kt]], [psb])
                tj = bi % 2
                act(TMP[:, tj, 0:n], pst[:, 0:n], AF.Sigmoid, [psb, BINF_b], [TMP_b[tj]],
                    bias=BINF[:, 24 + ct:25 + ct])
                tt(HG[:, m, c0:c0 + n], TMP[:, tj, 0:n], ZA[:, m, c0:c0 + n], ALU.mult, [TMP_b[tj], ZA_b[m]], [HG_b[m]])
            for i, hc in enumerate((0, NL + HB, NL + 2 * HB, NL + 2 * HB + HB + NCX)):
                tt(HG[:, m, hc:hc + HB], HG[:, m, hc:hc + HB], MASKB[:, i, :], ALU.mult, [HG_b[m], MASKB_b], [HG_b[m]])
            acc = ph_acc[m % 2]
            accbs = ph_acc_b[m % 2]
            segs = [(0, NL, 0), (NL + 2 * HB, NCX, NL)]
            for k in range(31):
                for si, (sb, n, cc0) in enumerate(segs):
                    accb = accbs[si]
                    src = HG[:, m, sb + k:sb + k + n]
                    if k == 0:
                        ts(acc[:, cc0:cc0 + n], src, BCW[:, ct, 0:1], BSM[:, ct, 0:1], ALU.mult, ALU.add,
                           [HG_b[m], BCW_b, BSM_b], [accb])
                    elif k < 30:
                        stt(acc[:, cc0:cc0 + n], src, BCW[:, ct, k:k + 1], acc[:, cc0:cc0 + n], ALU.mult, ALU.add,
                            [HG_b[m], BCW_b, accb], [accb])
                    else:
                        stt(YB[:, ct, cc0:cc0 + n], src, BCW[:, ct, k:k + 1], acc[:, cc0:cc0 + n], ALU.mult, ALU.add,
                            [HG_b[m], BCW_b, accb], [YB_b[ct]])
    dump("convb", YB[:, :, :], YB_b, [128, 8, NCEN], BF16)

    CB3 = [(0, 512), (512, 512), (1024, 128)]
    om1024, om1024b = onesm["m1024"]
    for (c0, n) in CB3:
        pst, psb = next_ps()
        for ct in range(8):
            mm(pst[:, 0:n], om1024[:, :], YB[:, ct, c0:c0 + n], ct == 0, ct == 7, [om1024b, YB_b[ct]], [psb])
        cp(RSTD[:, 0:n], pst[:, 0:n], [psb], [RSTD_b])
        pst2, psb2 = next_ps()
        for ct in range(8):
            tt(YB[:, ct, c0:c0 + n], YB[:, ct, c0:c0 + n], RSTD[:, 0:n], ALU.subtract, [YB_b[ct], RSTD_b], [YB_b[ct]])
            j = ct % 2
            act(SQ[:, j, 0:n], YB[:, ct, c0:c0 + n], AF.Square, [YB_b[ct]], [SQ_b[j]])
            mm(pst2[:, 0:n], om1024[:, :], SQ[:, j, 0:n], ct == 0, ct == 7, [om1024b, SQ_b[j]], [psb2])
        act(RSTD[:, 0:n], pst2[:, 0:n], AF.Sqrt, [psb2, EPSB_b], [RSTD_b], bias=EPSB[:, 0:1])
        recip(RSTD[:, 0:n], RSTD[:, 0:n], [RSTD_b], [RSTD_b])
        for ct in range(8):
            j = ct % 2
            tt(TMP[:, j, 0:n], YB[:, ct, c0:c0 + n], RSTD[:, 0:n], ALU.mult, [YB_b[ct], RSTD_b], [TMP_b[j]])
            act(YB[:, ct, c0:c0 + n], TMP[:, j, 0:n], AF.Silu, [TMP_b[j], BSM_b], [YB_b[ct]],
                bias=BSM[:, ct, 2:3], scale=BSM[:, ct, 1:2])
    dump("yb", YB[:, :, :], YB_b, [128, 8, NCEN], BF16)

    ada_finalize(0, 2, 6)
    ada_ab(0, 1)
    C.barrier(["pe", "act", "dve", "sp"])
    XS = ACC
    XS_b = [Buf(), Buf()]
    for s in range(8):
        W, Wb = next_w("wout")
        Wv = W[:, :].rearrange("p (k c) -> p k c", k=KT)
        for m in range(2):
            mt = s * 2 + m
            j = mt % 2
            load(XS[:, j, 0:NL], xT[mt * 128:(mt + 1) * 128, LOFF:LOFF + NL], [XS_b[j]])
            load(XS[:, j, NL:NCEN], xT[mt * 128:(mt + 1) * 128, COFF:COFF + NCX], [XS_b[j]])
            for (c0, n) in CB3:
                r = 0 if c0 < NL else 1
                pst, psb = next_ps()
                for kt in range(KT):
                    src, srcb = (YA, YA_b) if kt < 8 else (YB, YB_b)
                    mm(pst[:, 0:n], Wv[:, kt, m * 128:(m + 1) * 128], src[:, kt % 8, c0:c0 + n], kt == 0, kt == KT - 1,
                       [Wb, srcb[kt % 8]], [psb])
                stt(XL[:, mt, c0:c0 + n], pst[:, 0:n], modv(0, r, 2, mt), XS[:, j, c0:c0 + n], ALU.mult, ALU.add,
                    [psb, MOD_b, XS_b[j]], [XL_b[mt][c0 // 512]])
    dump("xl1", XL[:, :, :], XL_all, [128, KT, NCEN])
    if stop_after == "p1":
        finish()
        return nc, C

    C.barrier(["pe", "act", "dve", "sp"])

    def ffn(l, with_ctx, xch):
        ph.top = ph_base
        NF = (NL + 2) + ((NCX + 2) if with_ctx else 0)
        segs = [(0, 512, 1, 0), (512, 512, 513, 0)]
        if with_ctx:
            segs.append((1024, 128, NL + 3, 1))
        H2 = ph.alloc([128, KT, NF], BF16, "h2")
        H2_b = [Buf() for _ in range(KT)]
        GT = ph.alloc([128, 2, NF], F32, "gt")
        GT_b = [[Buf() for _ in range(3)] for _ in range(2)]
        CT = ph.alloc([128, 2, NF], F32, "ct")
        CT_b = [Buf(), Buf()]
        UT = ph.alloc([128, 2, NF], BF16, "ut")
        UT_b = [[Buf() for _ in range(3)] for _ in range(2)]
        AT = ph.alloc([128, 2, 2, NF], BF16, "at")
        AT_b = [[Buf(), Buf()], [Buf(), Buf()]]
        SQ = ph.alloc([128, 2, 512], BF16, "sq")
        SQ_b = [Buf(), Buf()]
        RSTD = ph.alloc([128, 512], F32, "rstd")
        RSTD_b = Buf()
        TMP = ph.alloc([128, 2, 512], F32, "tmp")
        TMP_b = [Buf(), Buf()]
        SND = ph.alloc([128, 4, KT], BF16, "snd")
        SND_b = Buf()
        RCV = ph.alloc([128, 8, 64], BF16, "rcv")
        RCV_b = Buf()
        HAL = ph.alloc([128, 4, KT], F32, "hal")
        HAL_b = Buf()
        for (x0, n, h0, r) in segs:
            rms_rstd([XL[:, kt, x0:x0 + n] for kt in range(KT)], [XL_b[kt][x0 // 512] for kt in range(KT)], n, "m2048",
                     SQ, SQ_b, RSTD[:, 0:n], RSTD_b)
            for kt in range(KT):
                j = kt % 2
                stt(TMP[:, j, 0:n], XL[:, kt, x0:x0 + n], AB[:, l, r, 1, kt:kt + 1], RSTD[:, 0:n], ALU.mult, ALU.mult,
                    [XL_b[kt][x0 // 512], AB_b, RSTD_b], [TMP_b[j]])
                act(H2[:, kt, h0:h0 + n], TMP[:, j, 0:n], AF.Identity, [TMP_b[j], MOD_b], [H2_b[kt]],
                    bias=modv(l, r, 3, kt))
        memset(SND[:, :, :], 0.0, [SND_b])
        bcols = [1, NL] + ([NL + 3, NL + 2 + NCX] if with_ctx else [])
        for i, c in enumerate(bcols):
            cp(SND[:, i, :], H2[:, :, c], H2_b, [SND_b])
        load(cc_in[xch].ap(), SND[:, :, :].rearrange("p a b -> p (a b)"), [cc_in_b[xch]], [SND_b])
        prefetch(NRING - 1)
        collective([list(range(8))], cc_in[xch], cc_in_b[xch], cc_out[xch], cc_out_b[xch])
        load(RCV[:, :, :], cc_out[xch].ap().rearrange("(r p) c -> p r c", p=128), [RCV_b], [cc_out_b[xch]])
        hsrc = [1, 0, 3, 2]
        hcols = [0, NL + 1] + ([NL + 2, NL + 3 + NCX] if with_ctx else [])
        for i, hc in enumerate(hcols):
            sl = hsrc[i]
            for rr in range(8):
                src = RCV[:, rr, sl * 16:(sl + 1) * 16]
                if rr == 0:
                    ts(HAL[:, i, :], src, OHS[:, i, 0:1], None, ALU.mult, None, [RCV_b, OHS_b], [HAL_b])
                else:
                    stt(HAL[:, i, :], src, OHS[:, i, rr:rr + 1], HAL[:, i, :], ALU.mult, ALU.add,
                        [RCV_b, OHS_b, HAL_b], [HAL_b])
            cp(H2[:, :, hc], HAL[:, i, :], [HAL_b], H2_b)
        dump(f"h2_{l}", H2[:, :, :], H2_b, [128, KT, NF], BF16)
        FB = col_blocks(NF, 3)
        def up(f):
            Wg, Wgb = next_w(f"f{l}g")
            Wgv = Wg[:, :].rearrange("p (k c) -> p k c", k=KT)
            for ft in range(2):
                for bi, (c0, n) in enumerate(FB):
                    pst, psb = next_ps()
                    for kt in range(KT):
                        mm(pst[:, 0:n], Wgv[:, kt, ft * 128:(ft + 1) * 128], H2[:, kt, c0:c0 + n], kt == 0, kt == KT - 1,
                           [Wgb, H2_b[kt]], [psb])
                    act(GT[:, ft, c0:c0 + n], pst[:, 0:n], AF.Copy, [psb], [GT_b[ft][bi]])
            Wu, Wub = next_w(f"f{l}u")
            Wuv = Wu[:, :].rearrange("p (k c) -> p k c", k=KT)
            for ft in range(2):
                for bi, (c0, n) in enumerate(FB):
                    pst, psb = next_ps()
                    for kt in range(KT):
                        mm(pst[:, 0:n], Wuv[:, kt, ft * 128:(ft + 1) * 128], H2[:, kt, c0:c0 + n], kt == 0, kt == KT - 1,
                           [Wub, H2_b[kt]], [psb])
                    act(UT[:, ft, c0:c0 + n], pst[:, 0:n], AF.Copy, [psb], [UT_b[ft][bi]])
            pa = f % 2
            n = NF - 2
            for step in range(5):
                for ft in range(2):
                    fi = f * 2 + ft
                    if step == 0:
                        ts(CT[:, ft, 1:1 + n], GT[:, ft, 0:n], FCW[:, l, fi, 0:1], FCW[:, l, fi, 3:4], ALU.mult, ALU.add,
                           GT_b[ft] + [FCW_b], [CT_b[ft]])
                    elif step in (1, 2):
                        stt(CT[:, ft, 1:1 + n], GT[:, ft, step:step + n], FCW[:, l, fi, step:step + 1], CT[:, ft, 1:1 + n],
                            ALU.mult, ALU.add, GT_b[ft] + [FCW_b, CT_b[ft]], [CT_b[ft]])
                    elif step == 3:
                        act(CT[:, ft, 1:1 + n], CT[:, ft, 1:1 + n], AF.Silu, [CT_b[ft]], [CT_b[ft]])
                    else:
                        tt(AT[:, pa, ft, 1:1 + n], CT[:, ft, 1:1 + n], UT[:, ft, 1:1 + n], ALU.mult,
                           [CT_b[ft]] + UT_b[ft], [AT_b[pa][ft]])

        def down(f):
            pa = f % 2
            Wd, Wdb = next_w(f"f{l}d")
            Wdv = Wd[:, :].rearrange("p (k c) -> p k c", k=2)
            for mt in range(KT):
                for (x0, n, h0, r) in segs:
                    pst, psb = next_ps()
                    for ft in range(2):
                        mm(pst[:, 0:n], Wdv[:, ft, mt * 128:(mt + 1) * 128], AT[:, pa, ft, h0:h0 + n], ft == 0, ft == 1,
                           [Wdb, AT_b[pa][ft]], [psb])
                    stt(XL[:, mt, x0:x0 + n], pst[:, 0:n], modv(l, r, 5, mt), XL[:, mt, x0:x0 + n], ALU.mult, ALU.add,
                        [psb, MOD_b, XL_b[mt][x0 // 512]], [XL_b[mt][x0 // 512]])

        for f in range(NFCH):
            up(f)
            if l == 0:
                ada_steps(1, 3 if f < 4 else 2)
            if f >= 1:
                down(f - 1)
        down(NFCH - 1)

    ffn(0, True, 0)
    ada_finalize(1, 0, 6)
    ada_ab(1, 0)
    ada_ab(1, 1)
    dump("xl2", XL[:, :, :], XL_all, [128, KT, NCEN])
    if stop_after == "p2":
        finish()
        return nc, C
    C.barrier(["pe", "act", "dve", "sp"])

    ph.top = ph_base
    CQN = ph.alloc([128, 6, NL], BF16, "cqn")
    CQN_b = [Buf() for _ in range(6)]
    ROPE = ph.alloc([128, 2, NL], F32, "rope")
    ROPE_b = Buf()
    load(ROPE[0:64, :, :], ropet.ap(), [ROPE_b])
    p34_base = ph.mark()
    CKVN = ph.alloc([128, 4, NCEN], BF16, "ckvn")
    CKVN_b = [Buf() for _ in range(4)]
    KPRO = ph.alloc([128, NCEN], BF16, "kpro")
    KPRO_b = Buf()
    H1 = ph.alloc([128, KT, 512], BF16, "h1")
    H1_b = [Buf() for _ in range(KT)]
    CQR = ph.alloc([128, 6, 512], F32, "cqr")
    CQR_b = [Buf() for _ in range(6)]
    CKR = ph.alloc([128, 4, 512], F32, "ckr")
    CKR_b = [Buf() for _ in range(4)]
    KP = ph.alloc([128, 2, 512], F32, "kp")
    KP_b = [Buf(), Buf()]
    SQ = ph.alloc([128, 2, 512], BF16, "sq")
    SQ_b = [Buf(), Buf()]
    RSTD = ph.alloc([128, 512], F32, "rstd")
    RSTD_b = Buf()
    TMP = ph.alloc([128, 2, 512], F32, "tmp")
    TMP_b = [Buf(), Buf()]
    for (x0, n, s0, s1) in MW_BLOCKS:
        r = 0 if x0 < NL else 1
        rms_rstd([XL[:, kt, x0:x0 + n] for kt in range(KT)], [XL_b[kt][x0 // 512] for kt in range(KT)], n, "m2048",
                 SQ, SQ_b, RSTD[:, 0:n], RSTD_b)
        for kt in range(KT):
            j = kt % 2
            stt(TMP[:, j, 0:n], XL[:, kt, x0:x0 + n], AB[:, 1, r, 0, kt:kt + 1], RSTD[:, 0:n], ALU.mult, ALU.mult,
                [XL_b[kt][x0 // 512], AB_b, RSTD_b], [TMP_b[j]])
            act(H1[:, kt, 0:n], TMP[:, j, 0:n], AF.Identity, [TMP_b[j], MOD_b], [H1_b[kt]], bias=modv(1, r, 0, kt))
        for s in range(s0, s1):
            W, Wb = next_w("mwin")
            Wv = W[:, :].rearrange("p (k c) -> p k c", k=KT)
            for m in range(2):
                g = s * 2 + m
                if g == 11:
                    continue
                if g < 10:
                    pst, psb = next_ps()
                    for kt in range(KT):
                        mm(pst[:, 0:n], Wv[:, kt, m * 128:(m + 1) * 128], H1[:, kt, 0:n], kt == 0, kt == KT - 1,
                           [Wb, H1_b[kt]], [psb])
                    if g < 6:
                        act(CQR[:, g, 0:n], pst[:, 0:n], AF.Copy, [psb], [CQR_b[g]])
                    else:
                        act(CKR[:, g - 6, 0:n], pst[:, 0:n], AF.Copy, [psb], [CKR_b[g - 6]])
                else:
                    for half in range(2):
                        pst, psb = next_ps()
                        for kt in range(KT):
                            mm(pst[0:64, 0:n], Wv[:, kt, half * 64:(half + 1) * 64], H1[:, kt, 0:n], kt == 0,
                               kt == KT - 1, [Wb, H1_b[kt]], [psb])
                        act(KP[0:64, half, 0:n], pst[0:64, 0:n], AF.Copy, [psb], [KP_b[half]])
        if r == 0:
            rms_rstd([CQR[:, k, 0:n] for k in range(6)], CQR_b, n, "m768", SQ, SQ_b, RSTD[:, 0:n], RSTD_b)
            for k in range(6):
                stt(CQN[:, k, x0:x0 + n], CQR[:, k, 0:n], QKG[:, k:k + 1], RSTD[:, 0:n], ALU.mult, ALU.mult,
                    [CQR_b[k], QKG_b, RSTD_b], [CQN_b[k]])
        rms_rstd([CKR[:, k, 0:n] for k in range(4)], CKR_b, n, "m512", SQ, SQ_b, RSTD[:, 0:n], RSTD_b)
        for k in range(4):
            stt(CKVN[:, k, x0:x0 + n], CKR[:, k, 0:n], QKG[:, 6 + k:7 + k], RSTD[:, 0:n], ALU.mult, ALU.mult,
                [CKR_b[k], QKG_b, RSTD_b], [CKVN_b[k]])
        if r == 0:
            tt(TMP[0:64, 0, 0:n], KP[0:64, 0, 0:n], ROPE[0:64, 0, x0:x0 + n], ALU.mult, [KP_b[0], ROPE_b], [TMP_b[0]])
            tt(TMP[0:64, 1, 0:n], KP[0:64, 1, 0:n], ROPE[0:64, 1, x0:x0 + n], ALU.mult, [KP_b[1], ROPE_b], [TMP_b[1]])
            tt(KPRO[0:64, x0:x0 + n], TMP[0:64, 0, 0:n], TMP[0:64, 1, 0:n], ALU.add, TMP_b, [KPRO_b])
        else:
            cp(KPRO[0:64, x0:x0 + n], KP[0:64, 0, 0:n], [KP_b[0]], [KPRO_b])
    dump("cqn", CQN[:, :, :], CQN_b, [128, 6, NL], BF16)
    dump("ckvn", CKVN[:, :, :], CKVN_b, [128, 4, NCEN], BF16)
    dump("kpro", KPRO[0:64, :], [KPRO_b], [64, NCEN], BF16)
    load(kv_in[0:512, :].rearrange("(k p) c -> p k c", p=128), CKVN[:, :, :], [kv_in_b], CKVN_b)
    load(kv_in[512:576, :], KPRO[0:64, :], [kv_in_b], [KPRO_b])
    prefetch(NRING - 1)
    collective([list(range(8))], kv_in, kv_in_b, kv_out, kv_out_b)
    C.barrier(["pe", "act", "dve", "sp"])

    ph.top = p34_base
    KVL = ph.alloc([128, 4, NKEY], BF16, "kvl")
    KVL_b = Buf()
    KPR = ph.alloc([128, NKEY], BF16, "kpr")
    KPR_b = Buf()
    p4_mark = ph.mark()
    SA = ph.alloc([128, 2, 2, NCEN], BF16, "sa")
    SA_b = [[Buf(), Buf()], [Buf(), Buf()]]
    TS = ph.alloc([128, 2, NCEN], F32, "tsel")
    TS_b = [Buf(), Buf()]
    idx = 0
    for jr in range(4):
        for kt in range(5):
            pp = idx % 2
            idx += 1
            npp = 128 if kt < 4 else 64
            r0 = kt * 128
            load(SA[0:npp, pp, 0, :], kv_out[jr * 640 + r0:jr * 640 + r0 + npp, :], [SA_b[pp][0]], [kv_out_b])
            load(SA[0:npp, pp, 1, :], kv_out[(4 + jr) * 640 + r0:(4 + jr) * 640 + r0 + npp, :], [SA_b[pp][1]],
                 [kv_out_b])
            parts = [(0, NL, jr * NL)] + ([(NL, NCX, 4 * NL + jr * NCX)] if jr < 2 else [])
            for (s0, n, d0) in parts:
                ts(TS[0:npp, pp, s0:s0 + n], SA[0:npp, pp, 0, s0:s0 + n], BSEL[0:npp, 0:1], None, ALU.mult, None,
                   [SA_b[pp][0], BSEL_b], [TS_b[pp]])
                if kt < 4:
                    dest, destb = KVL[:, kt, d0:d0 + n], KVL_b
                else:
                    dest, destb = KPR[0:64, d0:d0 + n], KPR_b
                stt(dest, SA[0:npp, pp, 1, s0:s0 + n], BSEL[0:npp, 1:2], TS[0:npp, pp, s0:s0 + n], ALU.mult, ALU.add,
                    [SA_b[pp][1], BSEL_b, TS_b[pp]], [destb])
    C.barrier(["pe", "act", "dve", "sp"])
    ph.top = p4_mark
    KH = ph.alloc([128, NKEY], BF16, "kh")
    KH_b = [Buf() for _ in range(9)]
    VH = ph.alloc([128, NKT, 128], BF16, "vh")
    VH_b = [Buf() for _ in range(9)]
    QN = ph.alloc([128, NL], BF16, "qn")
    QN_b = Buf()
    QP = ph.alloc([128, NL], BF16, "qp")
    QP_b = Buf()
    ET = ph.alloc([128, 3, 512], BF16, "et")
    ET_b = [Buf(), Buf(), Buf()]
    OH = ph.alloc([128, NL], BF16, "oh")
    OH_b = Buf()
    TQ = ph.alloc([128, 2, 512], F32, "tq")
    TQ_b = [Buf(), Buf()]
    RC = TQ[:, 0, :]
    RC_b = TQ_b[0]
    dump("kvl", KVL[:, :, :], [KVL_b], [128, 4, NKEY], BF16)
    dump("kpr", KPR[0:64, :], [KPR_b], [64, NKEY], BF16)
    if stop_after == "p3":
        finish()
        return nc, C
    PS_POOL[0] = 6
    ps_rr[0] = 0
    KB = col_blocks(NKEY)
    for h in range(16):
        WA, WAb = next_w("hA")
        WQ = WA[:, 0:1536].rearrange("p (k c) -> p k c", k=6)
        WKV = WA[:, 1536:2560].rearrange("p (k c) -> p k c", k=4)
        for bi, (c0, n) in enumerate(KB):
            pst, psb = next_ps()
            for kt in range(4):
                mm(pst[:, 0:n], WKV[:, kt, 0:128], KVL[:, kt, c0:c0 + n], kt == 0, kt == 3, [WAb, KVL_b], [psb])
            act(KH[:, c0:c0 + n], pst[:, 0:n], AF.Copy, [psb], [KH_b[bi]])
        for t0 in range(0, NKT, 4):
            tl = list(range(t0, min(t0 + 4, NKT)))
            pst, psb = next_ps()
            for jj, t in enumerate(tl):
                for kt in range(4):
                    mm(pst[:, jj * 128:(jj + 1) * 128], KVL[:, kt, t * 128:(t + 1) * 128], WKV[:, kt, 128:256],
                       kt == 0, kt == 3, [WAb, KVL_b], [psb])
            cp(VH[:, t0:t0 + len(tl), :], pst[:, 0:128 * len(tl)].rearrange("p (a b) -> p a b", b=128), [psb],
               [VH_b[t0 // 4]])
        for qb in range(2):
            q0 = qb * 512
            pst, psb = next_ps()
            for kt in range(6):
                mm(pst[:, :], WQ[:, kt, 0:128], CQN[:, kt, q0:q0 + 512], kt == 0, kt == 5, [WAb, CQN_b[kt]], [psb])
            act(QN[:, q0:q0 + 512], pst[:, :], AF.Copy, [psb], [QN_b])
            for half in range(2):
                pst, psb = next_ps()
                for kt in range(6):
                    mm(pst[0:64, :], WQ[:, kt, 128 + half * 64:192 + half * 64], CQN[:, kt, q0:q0 + 512], kt == 0,
                       kt == 5, [WAb, CQN_b[kt]], [psb])
                tt(TQ[0:64, half, :], pst[0:64, :], ROPE[0:64, half, q0:q0 + 512], ALU.mult, [psb, ROPE_b], [TQ_b[half]])
            tt(QP[0:64, q0:q0 + 512], TQ[0:64, 0, :], TQ[0:64, 1, :], ALU.add, TQ_b, [QP_b])
        if h == 0:
            dump("kh0", KH[:, :], KH_b, [128, NKEY], BF16)
            dump("vh0", VH[:, :, :], VH_b, [128, NKT, 128], BF16)
            dump("qn0", QN[:, :], [QN_b], [128, NL], BF16)
            dump("qp0", QP[0:64, :], [QP_b], [64, NL], BF16)
        for qb in range(2):
            q0 = qb * 512
            o_ps, o_b = ps_t[6], ps_b[6]
            d_ps, d_b = ps_t[7], ps_b[7]
            def scores(t):
                pst, psb = next_ps()
                mm(pst[:, :], KH[:, t * 128:(t + 1) * 128], QN[:, q0:q0 + 512], True, False, KH_b + [QN_b], [psb])
                mm(pst[:, :], KPR[0:64, t * 128:(t + 1) * 128], QP[0:64, q0:q0 + 512], False, True, [KPR_b, QP_b], [psb])
                return pst, psb
            cur = [scores(0), scores(1)]
            for t in range(NKT):
                pst, psb = cur.pop(0)
                if t + 2 < NKT:
                    cur.append(scores(t + 2))
                j = t % 3
                act(ET[:, j, :], pst[:, :], AF.Exp, [psb], [ET_b[j]], scale=ATT_SCALE)
                mm(o_ps[:, :], VH[:, t, :], ET[:, j, :], t == 0, t == NKT - 1, [VH_b[t // 4], ET_b[j]], [o_b])
                mm(d_ps[:, :], ones[:, :], ET[:, j, :], t == 0, t == NKT - 1, [ones_b, ET_b[j]], [d_b])
            recip(RC, d_ps[:, :], [d_b], [RC_b])
            tt(OH[:, q0:q0 + 512], o_ps[:, :], RC, ALU.mult, [o_b, RC_b], [OH_b])
        if h == 0:
            dump("oh0", OH[:, :], [OH_b], [128, NL], BF16)
        WB, WBb = next_w("hB")
        for mt in range(KT):
            for qb in range(2):
                q0 = qb * 512
                pst, psb = next_ps()
                mm(pst[:, :], WB[:, mt * 128:(mt + 1) * 128], OH[:, q0:q0 + 512], True, True, [WBb, OH_b], [psb])
                stt(XL[:, mt, q0:q0 + 512], pst[:, :], modv(1, 0, 2, mt), XL[:, mt, q0:q0 + 512], ALU.mult, ALU.add,
                    [psb, MOD_b, XL_b[mt][qb]], [XL_b[mt][qb]])
    PS_POOL[0] = 8
    dump("xl3", XL[:, :, 0:NL], XL_all, [128, KT, NL])
    if stop_after == "p4":
        finish()
        return nc, C
    C.barrier(["pe", "act", "dve", "sp"])

    ffn(1, False, 1)
    C.barrier(["pe", "act", "dve", "sp"])
    ph.top = ph_base
    SQ = ph.alloc([128, 2, 512], BF16, "sq")
    SQ_b = [Buf(), Buf()]
    RSTD = ph.alloc([128, 512], F32, "rstd")
    RSTD_b = Buf()
    OS = ph.alloc([128, 2, 512], F32, "os")
    OS_b = [Buf(), Buf()]
    for qb in range(2):
        q0 = qb * 512
        rms_rstd([XL[:, kt, q0:q0 + 512] for kt in range(KT)], [XL_b[kt][qb] for kt in range(KT)], 512, "m2048", SQ, SQ_b,
                 RSTD[:, :], RSTD_b)
        for kt in range(KT):
            j = kt % 2
            stt(OS[:, j, :], XL[:, kt, q0:q0 + 512], NG[:, 4, kt:kt + 1], RSTD[:, :], ALU.mult, ALU.mult,
                [XL_b[kt][qb], NG_b, RSTD_b], [OS_b[j]])
            load(outT[kt * 128:(kt + 1) * 128, q0:q0 + 512], OS[:, j, :], [Buf()], [OS_b[j]])
    finish()
    C.plan_rec = plan_rec
    return nc, C


def emit_program(nc, C):
    with nc.Block() as block:
        @block.tensor
        def _(e):
            C.replay(C.queues["pe"], e)

        @block.scalar
        def _(e):
            C.replay(C.queues["act"], e)

        @block.vector
        def _(e):
            C.replay(C.queues["dve"], e)

        @block.gpsimd
        def _(e):
            C.replay(C.queues["pool"], e)

        @block.sync
        def _(e):
            C.replay(C.queues["sp"], e)
    return nc


def _slots_kn(W, cols):
    K, N = W.shape
    kt, ns = K // 128, N // cols
    return np.ascontiguousarray(W.reshape(kt, 128, ns, cols).transpose(2, 1, 0, 3)).reshape(ns, 128, kt * cols)


def _fm(v):
    return np.ascontiguousarray(v.reshape(-1, 128).T)


def _bc(v):
    return np.ascontiguousarray(np.broadcast_to(v[None, :], (128, v.shape[0])))


def build_wstream(inp, order):
    ws = np.zeros((len(order), 128, SLOT), np.float32)
    src = {}
    src["ada0"] = list(_slots_kn(inp["w_ada"][0], 256))
    src["ada1"] = list(_slots_kn(inp["w_ada"][1], 256))
    w_in = inp["ab_w_in"][0]
    o = list(range(0, 8))
    for pr in range(4):
        o += [8 + pr, 12 + pr]
    win_slots = _slots_kn(w_in, 256)
    src["win"] = [win_slots[i] for i in o]
    src["wout"] = list(_slots_kn(inp["ab_w_out"][0], 256))
    for l in range(2):
        wup = inp["ffn_w_up"][l]
        src[f"f{l}g"] = list(_slots_kn(wup[:, :DFF], 256))
        src[f"f{l}u"] = list(_slots_kn(wup[:, DFF:], 256))
        src[f"f{l}d"] = list(inp["ffn_w_down"][l].reshape(NFCH, 2, 128, D).transpose(0, 2, 1, 3).reshape(NFCH, 128, 2 * D))
    mw = inp["mla_w_in"][0]
    kpe = mw[:, 1280:1344]
    mw2 = np.concatenate([mw, kpe[:, 32:64], kpe[:, 0:32], np.zeros((D, 128), np.float32)], axis=1)
    mslots = _slots_kn(mw2, 256)
    src["mwin"] = []
    for (_, _, s0, s1) in MW_BLOCKS:
        src["mwin"] += [mslots[s] for s in range(s0, s1)]
    wuq = inp["mla_w_uq"][0]
    wukv = inp["mla_w_ukv"][0]
    wo = inp["mla_w_o"][0]
    src["hA"], src["hB"] = [], []
    for h in range(16):
        qh = wuq[:, h * 192:(h + 1) * 192]
        uq = np.concatenate([qh[:, 0:128], qh[:, 128:192], qh[:, 160:192], qh[:, 128:160]], axis=1)
        ukv = wukv[:, h * 256:(h + 1) * 256]
        a = np.concatenate([uq.reshape(6, 128, 256).transpose(1, 0, 2).reshape(128, 1536),
                            ukv.reshape(4, 128, 256).transpose(1, 0, 2).reshape(128, 1024)], axis=1)
        src["hA"].append(a)
        src["hB"].append(np.ascontiguousarray(wo[h * 128:(h + 1) * 128, :]))
    pos = {k: 0 for k in src}
    for i, tag in enumerate(order):
        arr = src[tag][pos[tag]]
        pos[tag] += 1
        ws[i, :, :arr.shape[1]] = arr
    for k in src:
        assert pos[k] == len(src[k]), (k, pos[k], len(src[k]))
    return ws


def rope_tables():
    pos = np.arange(L)
    row = (pos // 64).astype(np.float32)
    col = (pos % 64).astype(np.float32)
    n_freq = 16
    inv = (np.float32(10000.0) ** (-np.arange(n_freq, dtype=np.float32) / n_freq)).astype(np.float32)
    ang = np.concatenate([row[:, None] * inv, col[:, None] * inv], axis=-1).astype(np.float32)
    cos, sin = np.cos(ang).astype(np.float32), np.sin(ang).astype(np.float32)
    cos2 = np.concatenate([cos, cos], axis=1).T
    sins = np.concatenate([-sin, sin], axis=1).T
    return cos2, sins


def build_in_maps(inp, order):
    inp = {k: np.asarray(v, dtype=np.float32) for k, v in inp.items()}
    ws = build_wstream(inp, order)
    cos2, sins = rope_tables()
    shared = {
        "wstream": ws,
        "bada": np.ascontiguousarray(np.stack([_fm(inp["b_ada"][0]), _fm(inp["b_ada"][1])], axis=1)),
        "ng": np.ascontiguousarray(np.stack([_fm(inp["norm1_g"][0]), _fm(inp["norm1_g"][1]), _fm(inp["norm2_g"][0]),
                                             _fm(inp["norm2_g"][1]), _fm(inp["final_norm_g"])], axis=1)),
        "bin_fm": _fm(inp["ab_b_in"][0]),
        "binv": _bc(inp["ab_b_in"][0][1024:2048]),
        "alng": _bc(inp["a_ln_g"][0]),
        "alnb": _bc(inp["a_ln_b"][0]),
        "wst": np.ascontiguousarray(inp["a_w_s"][0].transpose(2, 0, 1)),
        "bsb": np.ascontiguousarray(np.broadcast_to(inp["a_b_s"][0][None], (128, 8, 128))),
        "bcw": np.ascontiguousarray(inp["b_conv_w"][0].reshape(31, 8, 128).transpose(2, 1, 0)),
        "bsm": np.ascontiguousarray(np.stack([_fm(inp["b_conv_b"][0]), _fm(inp["b_ln_g"][0]), _fm(inp["b_ln_b"][0])],
                                             axis=2)),
        "fcw": np.ascontiguousarray(np.stack([
            np.concatenate([inp["ffn_conv_w"][l].reshape(3, 44, 128).transpose(2, 1, 0),
                            _fm(inp["ffn_conv_b"][l])[:, :, None]], axis=2) for l in range(2)], axis=1)),
        "qkg": np.ascontiguousarray(np.concatenate([_fm(inp["mla_q_norm_g"][0]), _fm(inp["mla_kv_norm_g"][0])], axis=1)),
    }
    x, ctx, c, c_ctx = inp["x"], inp["ctx"], inp["c"], inp["c_ctx"]
    maps = []
    for k in range(8):
        b, q = k // 4, k % 4
        cc = q % 2
        xe = np.zeros((NM, D), np.float32)
        t0, t1 = q * NL - HB, (q + 1) * NL + HB
        s0, s1 = max(t0, 0), min(t1, L)
        xe[s0 - t0:s1 - t0] = x[b, s0:s1]
        base = NL + 2 * HB
        t0, t1 = cc * NCX - HB, (cc + 1) * NCX + HB
        s0, s1 = max(t0, 0), min(t1, 256)
        xe[base + s0 - t0:base + s1 - t0] = ctx[b, s0:s1]
        maskb = np.zeros((128, 4, HB), np.float32)
        maskb[:, 0] = 1.0 if q > 0 else 0.0
        maskb[:, 1] = 1.0 if q < 3 else 0.0
        maskb[:, 2] = 1.0 if cc == 1 else 0.0
        maskb[:, 3] = 1.0 if cc == 0 else 0.0
        oh = np.zeros((128, 4, 8), np.float32)
        if q > 0:
            oh[:, 0, k - 1] = 1.0
        if q < 3:
            oh[:, 1, k + 1] = 1.0
        if cc == 1:
            oh[:, 2, 4 * b] = 1.0
        if cc == 0:
            oh[:, 3, 4 * b + 1] = 1.0
        m = dict(shared)
        m["xT"] = np.ascontiguousarray(xe.T)
        m["cmat"] = np.ascontiguousarray(np.stack([_fm(c[b]), _fm(c_ctx)], axis=2))
        m["maskb"] = maskb
        m["ohs"] = oh
        bs = np.zeros((128, 2), np.float32)
        bs[:, b] = 1.0
        m["bsel"] = bs
        m["ropet"] = np.ascontiguousarray(np.stack([cos2[:, q * NL:(q + 1) * NL], sins[:, q * NL:(q + 1) * NL]], axis=1))
        maps.append(m)
    return maps


_NC_CACHE = {}


def kernel(**inputs):
    if "nc" not in _NC_CACHE:
        _, C0 = build_nc()
        nc, C = build_nc(order=C0.plan_rec)
        emit_program(nc, C)
        _NC_CACHE["nc"] = nc
        _NC_CACHE["order"] = list(C0.plan_rec)
    nc = _NC_CACHE["nc"]
    maps = build_in_maps(inputs, _NC_CACHE["order"])
    res = run_bass_kernel_spmd(nc, maps, core_ids=list(range(8)))
    out = np.zeros((2, L, D), np.float32)
    for k in range(8):
        b, q = k // 4, k % 4
        out[b, q * NL:(q + 1) * NL, :] = res.results[k]["outT"].T
    return out
```
